# Optimizing a Trainium2 kernel written in Bass

```python
import jax, jax.numpy as jnp
from jax import lax
import numpy as np

D_MODEL = 1024
BATCH = 4
SEQ = 8192
DEPTH = 4

N_MIXERS = 2
N_A_LAYERS = (DEPTH + 1) // 2
N_B_LAYERS = DEPTH // 2
HGRN_EXPAND = 128
HGRN_HEADS = D_MODEL // HGRN_EXPAND
HGRN_HEAD_K = HGRN_EXPAND
HGRN_HEAD_V = D_MODEL // HGRN_HEADS
HGRN_CHUNK = 32
CONV_WIDTH = 31
D_FF = 4 * D_MODEL
DEEPNORM_ALPHA = (2.0 * DEPTH) ** 0.25
DEEPNORM_BETA = (8.0 * DEPTH) ** -0.25
LN_EPS = 1e-5
RMS_EPS = 1e-6
GATE_EPS = 1e-6

kernel_name = "hgrn2_conformer_interleaved_deepnorm"


def layer_norm(x, g, b):
    x32 = x.astype(jnp.float32)
    mu = jnp.mean(x32, axis=-1, keepdims=True)
    var = jnp.mean(jnp.square(x32 - mu), axis=-1, keepdims=True)
    y = (x32 - mu) * lax.rsqrt(var + LN_EPS) * g.astype(jnp.float32) + b.astype(jnp.float32)
    return y.astype(x.dtype)


def chunkwise_gated_recurrence(q, k, v, b):
    C = q.shape[-2]
    causal = jnp.tril(jnp.ones((C, C), dtype=bool))[:, :, None]

    def step(S, inp):
        qc, kc, vc, bc = inp
        diff = bc[..., :, None, :] - bc[..., None, :, :]
        decay = jnp.where(causal, jnp.exp(jnp.where(causal, diff, 0.0)), 0.0)
        scores = jnp.einsum('bhtd,bhsd,bhtsd->bhts', qc, kc, decay)
        o = (jnp.einsum('bhts,bhsv->bhtv', scores, vc)
             + jnp.einsum('bhtd,bhdv->bhtv', qc * jnp.exp(bc), S))
        b_last = bc[..., -1:, :]
        S = (jnp.exp(b_last)[..., 0, :, None] * S
             + jnp.einsum('bhsd,bhsv->bhdv', kc * jnp.exp(b_last - bc), vc))
        return S, o

    S0 = jnp.zeros(q.shape[1:3] + (q.shape[-1], v.shape[-1]), jnp.float32)
    _, o = lax.scan(step, S0, (q, k, v, b))
    return o


def hgrn2_mixer(h, w_in, lb, norm_g, w_out):
    B_, S_, D = h.shape
    H, dk, dv, C = HGRN_HEADS, HGRN_HEAD_K, HGRN_HEAD_V, HGRN_CHUNK
    nC = S_ // C
    proj = h @ w_in
    q, fz, v, g = jnp.split(proj, 4, axis=-1)
    q = jax.nn.silu(q.astype(jnp.float32))
    lb32 = lb.astype(jnp.float32)
    f = lb32 + (1.0 - lb32) * jax.nn.sigmoid(fz.astype(jnp.float32))
    log_f = jnp.log(jnp.maximum(f, GATE_EPS))
    k = 1.0 - f

    def to_chunks(t, d):
        return t.astype(jnp.float32).reshape(B_, nC, C, H, d).transpose(1, 0, 3, 2, 4)

    qc, kc, vc = to_chunks(q, dk), to_chunks(k, dk), to_chunks(v, dv)
    bc = jnp.cumsum(to_chunks(log_f, dk), axis=-2)
    o = chunkwise_gated_recurrence(qc, kc, vc, bc)
    o = o.transpose(1, 0, 3, 2, 4).reshape(B_, S_, H, dv)
    o = o * lax.rsqrt(jnp.mean(jnp.square(o), axis=-1, keepdims=True) + RMS_EPS)
    o = o * norm_g.astype(jnp.float32).reshape(H, dv)
    o = o.reshape(B_, S_, D) * jax.nn.silu(g.astype(jnp.float32))
    return o.astype(h.dtype) @ w_out


def conformer_conv_mixer(h, w_pw1, b_pw1, w_dw, b_dw, ln_g, ln_b, w_pw2, b_pw2):
    u = h @ w_pw1 + b_pw1
    a, gate = jnp.split(u, 2, axis=-1)
    u = a * jax.nn.sigmoid(gate)
    u = lax.conv_general_dilated(
        u, w_dw[:, None, :].astype(u.dtype), window_strides=(1,), padding=[(CONV_WIDTH - 1, 0)],
        dimension_numbers=('NWC', 'WIO', 'NWC'), feature_group_count=D_MODEL) + b_dw
    u = jax.nn.silu(layer_norm(u, ln_g, ln_b))
    return u @ w_pw2 + b_pw2


def setup_inputs(seed: int = 0) -> dict:
    key = jax.random.key(seed)
    ks = jax.random.split(key, 20)
    D, F, K = D_MODEL, D_FF, CONV_WIDTH
    beta = DEEPNORM_BETA

    def nrm(k, shape, scale):
        return jax.random.normal(k, shape, jnp.float32) * scale

    x = nrm(ks[0], (BATCH, SEQ, D), 1.0)
    ln_mix_g = 1.0 + nrm(ks[1], (DEPTH, D), 0.02)
    ln_mix_b = nrm(ks[2], (DEPTH, D), 0.02)
    ln_ffn_g = 1.0 + nrm(ks[3], (DEPTH, D), 0.02)
    ln_ffn_b = nrm(ks[4], (DEPTH, D), 0.02)
    ffn_w1 = nrm(ks[5], (DEPTH, D, F), D ** -0.5 * beta)
    ffn_w2 = nrm(ks[6], (DEPTH, F, D), F ** -0.5 * beta)
    col_scale = jnp.concatenate([jnp.ones((2 * D,), jnp.float32), jnp.full((D,), beta, jnp.float32),
                                 jnp.ones((D,), jnp.float32)])
    a_w_in = nrm(ks[7], (N_A_LAYERS, D, 4 * D), D ** -0.5) * col_scale
    a_lb_logits = nrm(ks[8], (N_A_LAYERS, D), 0.5)
    a_norm_g = 1.0 + nrm(ks[9], (N_A_LAYERS, D), 0.02)
    a_w_out = nrm(ks[10], (N_A_LAYERS, D, D), D ** -0.5 * beta)
    b_w_pw1 = nrm(ks[11], (N_B_LAYERS, D, 2 * D), D ** -0.5)
    b_b_pw1 = nrm(ks[12], (N_B_LAYERS, 2 * D), 0.02)
    b_w_dw = nrm(ks[13], (N_B_LAYERS, K, D), K ** -0.5)
    b_b_dw = nrm(ks[14], (N_B_LAYERS, D), 0.02)
    b_ln_g = 1.0 + nrm(ks[15], (N_B_LAYERS, D), 0.02)
    b_ln_b = nrm(ks[16], (N_B_LAYERS, D), 0.02)
    b_w_pw2 = nrm(ks[17], (N_B_LAYERS, D, D), D ** -0.5 * beta)
    b_b_pw2 = nrm(ks[18], (N_B_LAYERS, D), 0.02)
    return {"x": x, "ln_mix_g": ln_mix_g, "ln_mix_b": ln_mix_b, "ln_ffn_g": ln_ffn_g, "ln_ffn_b": ln_ffn_b,
            "ffn_w1": ffn_w1, "ffn_w2": ffn_w2,
            "a_w_in": a_w_in, "a_lb_logits": a_lb_logits, "a_norm_g": a_norm_g, "a_w_out": a_w_out,
            "b_w_pw1": b_w_pw1, "b_b_pw1": b_b_pw1, "b_w_dw": b_w_dw, "b_b_dw": b_b_dw,
            "b_ln_g": b_ln_g, "b_ln_b": b_ln_b, "b_w_pw2": b_w_pw2, "b_b_pw2": b_b_pw2}


def reference(x, ln_mix_g, ln_mix_b, ln_ffn_g, ln_ffn_b, ffn_w1, ffn_w2,
              a_w_in, a_lb_logits, a_norm_g, a_w_out,
              b_w_pw1, b_b_pw1, b_w_dw, b_b_dw, b_ln_g, b_ln_b, b_w_pw2, b_b_pw2):
    lb_soft = jax.nn.softmax(a_lb_logits.astype(jnp.float32), axis=0)
    lb_all = jnp.cumsum(lb_soft, axis=0) - lb_soft[0]
    for i in range(DEPTH):
        j = i // N_MIXERS
        if i % N_MIXERS == 0:
            mix = hgrn2_mixer(x, a_w_in[j], lb_all[j], a_norm_g[j], a_w_out[j])
        else:
            mix = conformer_conv_mixer(x, b_w_pw1[j], b_b_pw1[j], b_w_dw[j], b_b_dw[j],
                                       b_ln_g[j], b_ln_b[j], b_w_pw2[j], b_b_pw2[j])
        x = layer_norm(DEEPNORM_ALPHA * x + mix, ln_mix_g[i], ln_mix_b[i])
        ff = jnp.square(jax.nn.relu(x @ ffn_w1[i])) @ ffn_w2[i]
        x = layer_norm(DEEPNORM_ALPHA * x + ff, ln_ffn_g[i], ln_ffn_b[i])
    return x
```

```python
import numpy as np
from contextlib import ExitStack
import concourse.bass as bass
import concourse.mybir as mybir
from concourse.bass_utils import run_bass_kernel_spmd

F32 = mybir.dt.float32
BF16 = mybir.dt.bfloat16
AF = mybir.ActivationFunctionType
ALU = mybir.AluOpType

P = 128
D = 1024
KC = 8
DFF = 4096
T = 512
CH = 64
NCH = T // CH
KW = 31
SLAB = 4096
ALPHA = 8.0 ** 0.25
LN_EPS = 1e-5
RMS_EPS = 1e-6
GATE_EPS = 1e-6
NSLOT = 4
N_CORES = 4


class Buf:
    __slots__ = ("name", "w", "r", "arena")

    def __init__(self, name):
        self.name = name
        self.w = None
        self.r = []
        self.arena = False


def _flat(xs):
    out = []
    for b in xs:
        if isinstance(b, (tuple, list)):
            out.extend(_flat(b))
        else:
            out.append(b)
    return out


class Op:
    __slots__ = ("eng", "fn", "deps", "idx", "sig", "sigval", "is_dma", "key", "dma_val")


COMPUTE = ("tensor", "vector", "scalar", "gpsimd")
ENGS = ("sync",) + COMPUTE


class Sched:
    def __init__(self):
        self.ops = {e: [] for e in ENGS}
        self.last = {e: None for e in ENGS}
        self.pending = {e: set() for e in ENGS}
        self.dma_count = {}
        self.dma_last = {}
        self.arena_last = {e: None for e in ENGS}
        self.arena_pending = {e: set() for e in ENGS}

    def soft_barrier(self):
        snap = [self.arena_last[e] for e in COMPUTE if self.arena_last[e] is not None]
        for e in COMPUTE:
            self.arena_pending[e].update(snap)

    def add(self, eng, fn, reads=(), writes=(), dma_key=None, extra=()):
        reads = _flat(reads)
        writes = _flat(writes)
        op = Op()
        op.eng = eng
        op.fn = fn
        op.sig = False
        op.sigval = 0
        op.is_dma = dma_key is not None
        op.key = dma_key
        deps = set(extra)
        for b in reads:
            if b.w is not None:
                deps.add(b.w)
        for b in writes:
            if b.w is not None:
                deps.add(b.w)
            deps.update(b.r)
        for b in writes:
            b.w = op
            b.r = []
        for b in reads:
            b.r.append(op)
        if self.pending[eng]:
            deps.update(self.pending[eng])
            self.pending[eng] = set()
        if any(b.arena for b in reads) or any(b.arena for b in writes):
            if self.arena_pending[eng]:
                deps.update(self.arena_pending[eng])
                self.arena_pending[eng] = set()
            self.arena_last[eng] = op
        if op.is_dma:
            prev = self.dma_last.get(dma_key)
            if prev is not None:
                deps.add(prev)
            self.dma_count[dma_key] = self.dma_count.get(dma_key, 0) + 16
            op.dma_val = self.dma_count[dma_key]
            self.dma_last[dma_key] = op
        deps.discard(op)
        op.deps = deps
        op.idx = len(self.ops[eng])
        self.ops[eng].append(op)
        self.last[eng] = op
        return op

    def barrier(self):
        snap = [self.last[e] for e in COMPUTE if self.last[e] is not None]
        for e in COMPUTE:
            self.pending[e].update(o for o in snap if o.eng != e)

    def finalize(self):
        for e in ENGS:
            for op in self.ops[e]:
                for d in op.deps:
                    if not d.is_dma and (d.eng != e or e != "tensor"):
                        d.sig = True
        for e in ENGS:
            n = 0
            for op in self.ops[e]:
                if op.sig and not op.is_dma:
                    n += 1
                    op.sigval = n

    def emit(self, eng_name, eng, eng_sems, dma_sems):
        waited = {}
        for op in self.ops[eng_name]:
            need = {}
            for d in op.deps:
                if d.is_dma:
                    k = ("dma", d.key)
                    v = d.dma_val
                elif d.eng == eng_name and eng_name == "tensor":
                    continue
                else:
                    k = d.eng
                    v = d.sigval
                if need.get(k, 0) < v:
                    need[k] = v
            for k, v in need.items():
                if waited.get(k, 0) >= v:
                    continue
                waited[k] = v
                sem = dma_sems[k[1]] if isinstance(k, tuple) else eng_sems[k]
                eng.wait_ge(sem, v)
            ins = op.fn(eng)
            if ins is None:
                continue
            if op.is_dma:
                ins.then_inc(dma_sems[op.key], 16)
            elif op.sig:
                ins.then_inc(eng_sems[eng_name], 1)


def vec_layout(kinds):
    off = {}
    n = 0

    def put(name, cols):
        nonlocal n
        off[name] = n
        n += cols

    for i in range(len(kinds)):
        for nm in ("lnmg", "lnmb", "lnfg", "lnfb"):
            put(f"{nm}{i}", KC)
    put("lbl", 2 * KC)
    ja = jb = 0
    for k in kinds:
        if k == "A":
            put(f"ang{ja}", KC)
            ja += 1
        else:
            put(f"bpw1{jb}", 2 * KC)
            for nm in ("bdw", "blg", "blb", "bpw2"):
                put(f"{nm}{jb}", KC)
            put(f"wdw{jb}", KC * KW)
            jb += 1
    return off, n


def fm_vec(v):
    v = np.asarray(v, np.float32)
    return np.ascontiguousarray(v.reshape(-1, P).T)


def colslabs(w, ns):
    K, N = w.shape
    kc = K // P
    a = w.reshape(kc, P, N // ns, ns).transpose(2, 1, 0, 3)
    return np.ascontiguousarray(a).reshape(N // ns, P, kc * ns)


def pack_weights(kinds, layer_ids, inp):
    slabs = []
    seq = []
    conv_ids = {}
    off, nv = vec_layout(kinds)
    vecs = np.zeros((P, nv), np.float32)
    lbl = np.asarray(inp["a_lb_logits"], np.float32)
    vecs[:, off["lbl"]:off["lbl"] + 2 * KC] = fm_vec(lbl.reshape(-1))
    n_conv = 0
    pending_conv = []
    ja = jb = 0
    for li, k in enumerate(kinds):
        i = layer_ids[li]
        j = i // 2
        vecs[:, off[f"lnmg{li}"]:off[f"lnmg{li}"] + KC] = fm_vec(inp["ln_mix_g"][i])
        vecs[:, off[f"lnmb{li}"]:off[f"lnmb{li}"] + KC] = fm_vec(inp["ln_mix_b"][i])
        vecs[:, off[f"lnfg{li}"]:off[f"lnfg{li}"] + KC] = fm_vec(inp["ln_ffn_g"][i])
        vecs[:, off[f"lnfb{li}"]:off[f"lnfb{li}"] + KC] = fm_vec(inp["ln_ffn_b"][i])
        if k == "A":
            w_in = np.asarray(inp["a_w_in"][j], np.float32)
            a = w_in.reshape(KC, P, 4, KC, P).transpose(3, 1, 0, 2, 4)
            a = np.ascontiguousarray(a).reshape(KC, P, SLAB)
            for h in range(KC):
                seq.append(len(slabs))
                slabs.append(a[h])
            for s in colslabs(np.asarray(inp["a_w_out"][j], np.float32), 512):
                seq.append(len(slabs))
                slabs.append(s)
            vecs[:, off[f"ang{ja}"]:off[f"ang{ja}"] + KC] = fm_vec(inp["a_norm_g"][j])
            ja += 1
        else:
            w1 = np.asarray(inp["b_w_pw1"][j], np.float32)
            a = w1.reshape(KC, P, 2, 4, 2, P).transpose(3, 1, 0, 2, 4, 5)
            a = np.ascontiguousarray(a).reshape(4, P, SLAB)
            for s in range(4):
                seq.append(len(slabs))
                slabs.append(a[s])
            for c in range(KC):
                seq.append(("conv", jb, c))
            for s in colslabs(np.asarray(inp["b_w_pw2"][j], np.float32), 512):
                seq.append(len(slabs))
                slabs.append(s)
            vecs[:, off[f"bpw1{jb}"]:off[f"bpw1{jb}"] + 2 * KC] = fm_vec(inp["b_b_pw1"][j])
            vecs[:, off[f"bdw{jb}"]:off[f"bdw{jb}"] + KC] = fm_vec(inp["b_b_dw"][j])
            vecs[:, off[f"blg{jb}"]:off[f"blg{jb}"] + KC] = fm_vec(inp["b_ln_g"][j])
            vecs[:, off[f"blb{jb}"]:off[f"blb{jb}"] + KC] = fm_vec(inp["b_ln_b"][j])
            vecs[:, off[f"bpw2{jb}"]:off[f"bpw2{jb}"] + KC] = fm_vec(inp["b_b_pw2"][j])
            wdw = np.asarray(inp["b_w_dw"][j], np.float32)
            vecs[:, off[f"wdw{jb}"]:off[f"wdw{jb}"] + KC * KW] = np.ascontiguousarray(
                wdw.reshape(KW, KC, P).transpose(2, 1, 0)).reshape(P, KC * KW)
            jb += 1
        for s in colslabs(np.asarray(inp["ffn_w1"][i], np.float32), 512):
            seq.append(len(slabs))
            slabs.append(s)
        for s in colslabs(np.asarray(inp["ffn_w2"][i], np.float32), 128):
            seq.append(len(slabs))
            slabs.append(s)
    nh = len(slabs)
    seq2 = []
    for s in seq:
        if isinstance(s, tuple):
            seq2.append(nh + s[1] * KC + s[2])
        else:
            seq2.append(s)
    return np.stack(slabs), vecs, seq2, nh


def slab_seq(kinds):
    seq = []
    n = 0
    jb = 0
    for k in kinds:
        if k == "A":
            cnt = [8, 2]
        else:
            cnt = [4, ("conv", jb), 2]
            jb += 1
        for c in cnt + [8, 8]:
            if isinstance(c, tuple):
                seq.extend(("conv", c[1], x) for x in range(KC))
            else:
                seq.extend(range(n, n + c))
                n += c
    nh = n
    return [nh + s[1] * KC + s[2] if isinstance(s, tuple) else s for s in seq], nh


def build(n_tiles, kinds):
    nc = bass.Bass("TRN2", target_bir_lowering=False)
    L = len(kinds)
    nA = sum(1 for k in kinds if k == "A")
    nB = L - nA
    voff, NV = vec_layout(kinds)
    seq, NH = slab_seq(kinds)
    NS = NH + nB * KC

    x_in = nc.dram_tensor("x_fm", [n_tiles, P, KC * T], F32, kind="ExternalInput").ap()
    w_in_d = nc.dram_tensor("wslabs", [NH, P, SLAB], F32, kind="ExternalInput").ap()
    vecs_d = nc.dram_tensor("vecs", [P, NV], F32, kind="ExternalInput").ap()
    out_d = nc.dram_tensor("out_fm", [n_tiles, P, KC * T], F32, kind="ExternalOutput").ap()
    wsc = nc.dram_tensor("wsc", [NS, P, SLAB], BF16).ap()

    S = Sched()
    es = ExitStack()

    def sb(name, shape, dt):
        return es.enter_context(nc.sbuf_tensor(name, shape, dt))

    xres = sb("xres", [P, KC, T], F32)
    xbf = sb("xbf", [P, KC, T], BF16)
    zt = sb("zt", [P, KC, T], F32)
    zb = sb("zb", [P, KC, T], BF16)
    zsq = sb("zsq", [P, KC, T], BF16)
    st_mean = sb("st_mean", [P, T], F32)
    st_tmp = sb("st_tmp", [P, T], F32)
    st_rstd = sb("st_rstd", [P, T], F32)
    wslot = [sb(f"wslot{i}", [P, SLAB], BF16) for i in range(NSLOT)]
    vecs = sb("vecs_sb", [P, NV], F32)
    lbv = sb("lbv", [P, 4, KC], F32)
    ident_f = sb("ident_f", [P, P], F32)
    ident_b = sb("ident_b", [P, P], BF16)
    ones_d = sb("ones_d", [P, P], BF16)
    ones_h = sb("ones_h", [P, P], BF16)
    cmask = sb("cmask", [P, 1, CH], F32)
    epsc = sb("epsc", [P, 2], F32)
    carry = sb("carry", [P, T], F32)
    Sst = [sb(f"Sst{j}", [P, KC, P], F32) for j in range(max(nA, 1))]
    Sbf = [sb(f"Sbf{j}", [P, KC, P], BF16) for j in range(max(nA, 1))]
    halo = [sb(f"halo{j}", [P, KC, KW - 1], BF16) for j in range(max(nB, 1))]
    dec = sb("dec", [P, KC, NCH], F32)
    NBF = 30720
    NF32 = 7168
    abf = sb("abf", [P, NBF], BF16)
    af32 = sb("af32", [P, NF32], F32)

    ps_all = es.enter_context(nc.psum_tensor("ps_all", [P, 8, T], F32))
    banks = [ps_all[:, i, :] for i in range(8)]
    bank_buf = [Buf(f"bank{i}") for i in range(8)]

    class Rot:
        def __init__(self, ids):
            self.ids = ids
            self.i = 0

        def next(self):
            b = self.ids[self.i % len(self.ids)]
            self.i += 1
            return banks[b], bank_buf[b]

    def bl(name, n):
        return [Buf(f"{name}{i}") for i in range(n)]

    B_xres = bl("xres", KC)
    B_xbf = bl("xbf", KC)
    B_z = bl("z", KC)
    B_zb = bl("zb", KC)
    B_zsq = bl("zsq", KC)
    B_mean = Buf("mean")
    B_tmp = Buf("sttmp")
    B_rstd = Buf("rstd")
    B_slot = bl("slot", NSLOT)
    B_vecs = Buf("vecs")
    B_const = Buf("const")
    B_cast = bl("cast", L)
    B_conv = {}
    B_S = [bl(f"S{j}_", KC) for j in range(max(nA, 1))]
    B_Sbg = [bl(f"Sbf{j}_", 2) for j in range(max(nA, 1))]
    B_halo = bl("halo", max(nB, 1))
    B_dec = bl("dec", KC)

    def mm(out, lhsT, rhs, start, stop, reads, writes):
        return S.add("tensor", lambda e: e.matmul(out, lhsT=lhsT, rhs=rhs, start=start, stop=stop), reads, writes)

    def act(out, in_, func, reads, writes, bias=None, scale=None):
        kw = {}
        if bias is not None:
            kw["bias"] = bias
        if scale is not None:
            kw["scale"] = scale
        return S.add("scalar", lambda e: e.activation(out=out, in_=in_, func=func, **kw), reads, writes)

    def tt(eng, out, in0, in1, op, reads, writes):
        return S.add(eng, lambda e: e.tensor_tensor(out=out, in0=in0, in1=in1, op=op), reads, writes)

    def ts(eng, out, in0, s1, s2, op0, op1, reads, writes):
        if op1 is None:
            return S.add(eng, lambda e: e.tensor_scalar(out, in0, s1, None, op0), reads, writes)
        return S.add(eng, lambda e: e.tensor_scalar(out, in0, s1, s2, op0, op1), reads, writes)

    def stt(out, in0, scalar, in1, op0, op1, reads, writes):
        return S.add("vector", lambda e: e.scalar_tensor_tensor(out=out, in0=in0, scalar=scalar, in1=in1,
                                                                 op0=op0, op1=op1), reads, writes)

    def cp(eng, out, in_, reads, writes):
        if eng == "scalar":
            return act(out, in_, AF.Copy, reads, writes)
        return S.add(eng, lambda e: e.tensor_copy(out=out, in_=in_), reads, writes)

    def rsqrt(out, in_, epsi, reads, writes):
        act(out, in_, AF.Ln, list(reads) + [B_const], writes, bias=epsc[:, epsi:epsi + 1])
        act(out, out, AF.Exp, writes, writes, scale=-0.5)

    def memset(eng, ap, val, writes):
        return S.add(eng, lambda e: e.memset(ap, val), (), writes)

    def vcol(name, c=0):
        o = voff[name] + c
        return vecs[:, o:o + 1]

    wstate = {"issued": 0, "taken": 0}
    total_loads = n_tiles * len(seq)

    def slab_src_buf(sidx):
        if sidx >= NH:
            return B_conv[sidx]
        return B_piece[slab_piece[sidx]]

    slab_layer = {}
    _n = 0
    for li, k in enumerate(kinds):
        cnt = (8 + 2 if k == "A" else 4 + 2) + 16
        for s in range(_n, _n + cnt):
            slab_layer[s] = li
        _n += cnt
    assert _n == NH

    def issue_load(j):
        sidx = seq[j % len(seq)]
        slot = j % NSLOT
        if sidx >= NH:
            dst = wslot[slot][:, 0:KW * P]
            src = wsc[sidx][:, 0:KW * P]
        else:
            dst = wslot[slot][:]
            src = wsc[sidx]
        S.add("sync", lambda e: e.dma_start(out=dst, in_=src), [slab_src_buf(sidx)], [B_slot[slot]],
              dma_key=f"slot{slot}")

    def wget(la=NSLOT):
        i = wstate["taken"]
        while wstate["issued"] < min(i + la, total_loads):
            issue_load(wstate["issued"])
            wstate["issued"] += 1
        wstate["taken"] += 1
        return wslot[i % NSLOT], B_slot[i % NSLOT]

    S.add("gpsimd", lambda e: e.dma_start(out=vecs[:], in_=vecs_d), (), [B_vecs], dma_key="vecs")
    HT = KC * T // 2
    zbq = [zb[:].rearrange("p c t -> p (c t)").bitcast(F32), zsq[:].rearrange("p c t -> p (c t)").bitcast(F32)]

    def issue_xload(it):
        S.add("gpsimd", (lambda it: lambda e: e.dma_start(out=zbq[0], in_=x_in[it][:, 0:HT]))(it), (), B_zb, dma_key="xin0")
        S.add("gpsimd", (lambda it: lambda e: e.dma_start(out=zbq[1], in_=x_in[it][:, HT:2 * HT]))(it), (), B_zsq,
              dma_key="xin1")

    issue_xload(0)
    pieces = []
    lo = 0
    for li, k in enumerate(kinds):
        cnt = (8 + 2 if k == "A" else 4 + 2) + 16
        if li == 0:
            cuts = [lo, lo + 1, lo + 4, lo + 10, lo + cnt]
        else:
            cuts = [lo, lo + cnt]
        for a, b in zip(cuts[:-1], cuts[1:]):
            pieces.append((a, b))
        lo += cnt
    B_piece = bl("castp", len(pieces))
    slab_piece = {}
    piece_layer = []
    lo_ = 0
    for li, k in enumerate(kinds):
        cnt = (8 + 2 if k == "A" else 4 + 2) + 16
        for pi, (a, b) in enumerate(pieces):
            if lo_ <= a < lo_ + cnt:
                piece_layer.append((pi, li))
        lo_ += cnt

    def issue_cast(pi):
        a, b = pieces[pi]
        for sidx in range(a, b):
            slab_piece[sidx] = pi
        S.add("gpsimd", (lambda a, b: lambda e: e.dma_start(
            out=wsc[a:b].rearrange("s p n -> (s p) n"), in_=w_in_d[a:b].rearrange("s p n -> (s p) n")))(a, b),
            (), [B_piece[pi]], dma_key=f"cast{pi}")

    for pi, li in piece_layer:
        if li == 0:
            issue_cast(pi)
    memset("vector", ident_f[:], 1.0, [B_const])
    S.add("gpsimd", lambda e: e.affine_select(out=ident_f[:], in_=ident_f[:], pattern=[[-1, P]],
                                              compare_op=ALU.is_equal, fill=0.0, base=0, channel_multiplier=1),
          [B_const], [B_const])
    cp("vector", ident_b[:], ident_f[:], [B_const], [B_const])
    memset("vector", ones_d[:], 1.0 / D, [B_const])
    memset("vector", ones_h[:], 1.0 / P, [B_const])
    memset("vector", cmask[:], 1.0, [B_const])
    for hf in range(2):
        S.add("gpsimd", (lambda hf: lambda e: e.affine_select(
            out=cmask[hf * CH:(hf + 1) * CH, 0, :], in_=cmask[hf * CH:(hf + 1) * CH, 0, :], pattern=[[1, CH]],
            compare_op=ALU.is_ge, fill=0.0, base=0, channel_multiplier=-1))(hf), [B_const], [B_const])
    memset("vector", epsc[:, 0:1], LN_EPS, [B_const])
    memset("vector", epsc[:, 1:2], RMS_EPS, [B_const])
    memset("vector", carry[:], 1.0, [B_const])
    memset("vector", carry[:].rearrange("p (c j) -> p c j", j=CH)[:, :, 0:1], 0.0, [B_const])
    for j in range(max(nA, 1)):
        memset("vector", Sst[j][:], 0.0, B_S[j])
        memset("gpsimd", Sbf[j][:], 0.0, B_Sbg[j])
    for j in range(max(nB, 1)):
        memset("gpsimd", halo[j][:], 0.0, [B_halo[j]])
    B_lb = Buf("lb")
    l0 = vecs[:, voff["lbl"]:voff["lbl"] + KC]
    l1 = vecs[:, voff["lbl"] + KC:voff["lbl"] + 2 * KC]
    tt("vector", lbv[:, 1, :], l1, l0, ALU.subtract, [B_vecs], [B_lb])
    act(lbv[:, 2, :], lbv[:, 1, :], AF.Sigmoid, [B_lb], [B_lb])
    act(lbv[:, 3, :], lbv[:, 1, :], AF.Sigmoid, [B_lb], [B_lb], scale=-1.0)
    tt("vector", lbv[:, 0, :], lbv[:, 3, :], lbv[:, 3, :], ALU.subtract, [B_lb], [B_lb])
    ts("vector", lbv[:, 1, :], lbv[:, 0, :], -1.0, 1.0, ALU.mult, ALU.add, [B_lb], [B_lb])
    ts("vector", lbv[:, 3, :], lbv[:, 2, :], -1.0, 1.0, ALU.mult, ALU.add, [B_lb], [B_lb])
    dgs = sb("dgs", [P, 16 * P], BF16)
    B_dgs = Buf("dgs")
    diag_jobs = []
    for jb in range(nB):
        for c in range(KC):
            sidx = NH + jb * KC + c
            B_conv[sidx] = Buf(f"conv{sidx}")
            for half in range(2):
                def job(jb=jb, c=c, sidx=sidx, half=half):
                    k0 = 16 * half
                    nk = 16 if half == 0 else KW - 16
                    o = voff[f"wdw{jb}"] + c * KW + k0
                    tt("vector", dgs[:, 0:nk * P].rearrange("p (k m) -> p k m", m=P),
                       ident_f[:].unsqueeze(1).to_broadcast([P, nk, P]),
                       vecs[:, o:o + nk].unsqueeze(2).to_broadcast([P, nk, P]), ALU.mult,
                       [B_vecs, B_const], [B_dgs])
                    S.add("scalar", lambda e: e.dma_start(out=wsc[sidx][:, k0 * P:(k0 + nk) * P], in_=dgs[:, 0:nk * P]),
                          [B_dgs], [B_conv[sidx]], dma_key="dgs")
                diag_jobs.append(job)
    S.barrier()

    stat1 = (banks[6], bank_buf[6])
    stat2 = (banks[7], bank_buf[7])

    pend = []

    def ln_feed(d):
        act(zb[:, d, :], zt[:, d, :], AF.Copy, [B_z[d]], [B_zb[d]])
        act(zsq[:, d, :], zt[:, d, :], AF.Square, [B_z[d]], [B_zsq[d]])
        pend.append(d)

    def flush_stats():
        for d in pend:
            mm(stat1[0][:], ones_d[:], zb[:, d, :], d == 0, d == KC - 1, [B_zb[d], B_const], [stat1[1]])
            mm(stat2[0][:], ones_d[:], zsq[:, d, :], d == 0, d == KC - 1, [B_zsq[d], B_const], [stat2[1]])
        pend.clear()

    def ln_finish(gname, bname, mode, dst=None, B_dst=None, hook=None):
        flush_stats()
        if hook is not None:
            hook()
        act(st_tmp[:], stat1[0][:], AF.Square, [stat1[1]], [B_tmp])
        tt("vector", st_tmp[:], stat2[0][:], st_tmp[:], ALU.subtract, [stat2[1], B_tmp], [B_tmp])
        rsqrt(st_rstd[:], st_tmp[:], 0, [B_tmp], [B_rstd])
        for d in range(KC):
            tt("vector", zt[:, d, :], zt[:, d, :], stat1[0][:], ALU.subtract, [B_z[d], stat1[1]], [B_z[d]])
            tt("vector", zt[:, d, :], zt[:, d, :], st_rstd[:], ALU.mult, [B_z[d], B_rstd], [B_z[d]])
        order = list(range(KC))
        for d in order:
            if mode == "resid":
                act(xbf[:, d, :], zt[:, d, :], AF.Identity, [B_z[d], B_vecs], [B_xbf[d]],
                    bias=vcol(bname, d), scale=vcol(gname, d))
            else:
                act(dst[:, d, :], zt[:, d, :], AF.Silu, [B_z[d], B_vecs], [B_dst[d]],
                    bias=vcol(bname, d), scale=vcol(gname, d))
        if mode == "resid":
            for d in order:
                if d % 2 == 0:
                    act(xres[:, d, :], zt[:, d, :], AF.Identity, [B_z[d], B_vecs], [B_xres[d]],
                        bias=vcol(bname, d), scale=vcol(gname, d))
                else:
                    ts("gpsimd", xres[:, d, :], zt[:, d, :], vcol(gname, d), vcol(bname, d), ALU.mult, ALU.add,
                       [B_z[d], B_vecs], [B_xres[d]])

    def resid_from_psum(d, bank, bbuf):
        stt(zt[:, d, :], xres[:, d, :], ALPHA, bank[:], ALU.mult, ALU.add, [B_xres[d], bbuf], [B_z[d]])
        ln_feed(d)

    hT = abf[:, 0:32 * T].rearrange("p (c t) -> p c t", t=T)
    B_hT = bl("hT", 32)
    ftmp = [af32[:, i * T:(i + 1) * T] for i in range(3)]
    B_ftmp = bl("ftmp", 3)

    def ffn(li, hook=None):
        S.soft_barrier()
        rot = Rot([0, 1, 2, 3, 4, 5])
        n = 0
        NW = 6
        slabs = [wget(), wget(la=NSLOT - 1)]

        def evac(f, bank, bbuf):
            nonlocal n
            tb = n % 3
            act(ftmp[tb], bank[:], AF.Relu, [bbuf], [B_ftmp[tb]])
            tt("vector", hT[:, f, :], ftmp[tb], ftmp[tb], ALU.mult,
               [B_ftmp[tb]], [B_hT[f]])
            n += 1
            if diag_jobs:
                diag_jobs.pop(0)()

        wave = []
        for f in range(NW):
            w, wb = slabs[f // 4]
            w3 = w[:].rearrange("p (k n) -> p k n", n=512)
            bank, bbuf = rot.next()
            wave.append((f, w3, wb, bank, bbuf))
        for k in range(KC):
            for f, w3, wb, bank, bbuf in wave:
                j = f % 4
                mm(bank[:], w3[:, k, j * P:(j + 1) * P], xbf[:, k, :], k == 0, k == KC - 1, [wb, B_xbf[k]], [bbuf])
        for f, w3, wb, bank, bbuf in wave:
            evac(f, bank, bbuf)
        for s in range(1, 8):
            w, wb = slabs[s] if s < 2 else wget()
            w3 = w[:].rearrange("p (k n) -> p k n", n=512)
            for j in range(4):
                f = 4 * s + j
                if f < NW:
                    continue
                bank, bbuf = rot.next()
                for k in range(KC):
                    mm(bank[:], w3[:, k, j * P:(j + 1) * P], xbf[:, k, :], k == 0, k == KC - 1, [wb, B_xbf[k]], [bbuf])
                evac(f, bank, bbuf)
        for d in range(KC):
            w, wb = wget()
            w3 = w[:].rearrange("p (k n) -> p k n", n=P)
            bank, bbuf = rot.next()
            for f in range(32):
                mm(bank[:], w3[:, f, :], hT[:, f, :], f == 0, f == 31, [wb, B_hT[f]], [bbuf])
            flush_stats()
            resid_from_psum(d, bank, bbuf)
        ln_finish(f"lnfg{li}", f"lnfb{li}", "resid", hook=hook)

    UW = T + KW - 1
    ubuf = abf[:, 0:KC * UW].rearrange("p (c t) -> p c t", t=UW)
    cbuf = abf[:, KC * UW:KC * UW + KC * T].rearrange("p (c t) -> p c t", t=T)
    B_u = bl("u", KC)
    B_cb = bl("cb", KC)
    ctmp = [af32[:, i * T:(i + 1) * T] for i in range(3)]
    B_ctmp = bl("ctmp", 3)

    def conv_mixer(li, jb):
        while diag_jobs:
            diag_jobs.pop(0)()
        S.soft_barrier()
        rot = Rot([0, 1, 2, 3, 4, 5])
        cp("gpsimd", ubuf[:, :, 0:KW - 1], halo[jb][:], [B_halo[jb]], B_u)
        n = 0
        for s in range(4):
            w, wb = wget()
            w4 = w[:].rearrange("p (k q n) -> p k q n", q=4, n=P)
            grp = []
            for j in range(2):
                ba, bab = rot.next()
                bg, bgb = rot.next()
                grp.append((j, ba, bab, bg, bgb))
            if s == 0:
                for k in range(KC):
                    for j, ba, bab, bg, bgb in grp:
                        mm(ba[:], w4[:, k, j, :], xbf[:, k, :], k == 0, k == KC - 1, [wb, B_xbf[k]], [bab])
                        mm(bg[:], w4[:, k, 2 + j, :], xbf[:, k, :], k == 0, k == KC - 1, [wb, B_xbf[k]], [bgb])
            for j, ba, bab, bg, bgb in grp:
                c = 2 * s + j
                if s != 0:
                    for k in range(KC):
                        mm(ba[:], w4[:, k, j, :], xbf[:, k, :], k == 0, k == KC - 1, [wb, B_xbf[k]], [bab])
                    for k in range(KC):
                        mm(bg[:], w4[:, k, 2 + j, :], xbf[:, k, :], k == 0, k == KC - 1, [wb, B_xbf[k]], [bgb])
                tb = n % 3
                n += 1
                act(ctmp[tb], bg[:], AF.Sigmoid, [bgb, B_vecs], [B_ctmp[tb]], bias=vcol(f"bpw1{jb}", KC + c))
                stt(ubuf[:, c, KW - 1:UW], ba[:], vcol(f"bpw1{jb}", c), ctmp[tb], ALU.add, ALU.mult,
                    [bab, B_ctmp[tb], B_vecs], [B_u[c]])
        for c in range(KC):
            w, wb = wget()
            w3 = w[:, 0:KW * P].rearrange("p (k n) -> p k n", n=P)
            bank, bbuf = rot.next()
            for k in range(KW):
                mm(bank[:], w3[:, k, :], ubuf[:, c, k:k + T], k == 0, k == KW - 1, [wb, B_u[c]], [bbuf])
            flush_stats()
            act(zt[:, c, :], bank[:], AF.Identity, [bbuf, B_vecs], [B_z[c]], bias=vcol(f"bdw{jb}", c))
            ln_feed(c)
        cp("gpsimd", halo[jb][:], ubuf[:, :, T:UW], B_u, [B_halo[jb]])
        ln_finish(f"blg{jb}", f"blb{jb}", "conv", cbuf, B_cb)
        n = 0
        for s in range(2):
            w, wb = wget()
            w3 = w[:].rearrange("p (k n) -> p k n", n=512)
            grp = [(j,) + rot.next() for j in range(4)]
            if s == 0:
                for k in range(KC):
                    for j, bank, bbuf in grp:
                        mm(bank[:], w3[:, k, j * P:(j + 1) * P], cbuf[:, k, :], k == 0, k == KC - 1, [wb, B_cb[k]], [bbuf])
            for j, bank, bbuf in grp:
                d = 4 * s + j
                if s != 0:
                    for k in range(KC):
                        mm(bank[:], w3[:, k, j * P:(j + 1) * P], cbuf[:, k, :], k == 0, k == KC - 1, [wb, B_cb[k]], [bbuf])
                flush_stats()
                tb = n % 3
                n += 1
                act(ctmp[tb], bank[:], AF.Identity, [bbuf, B_vecs], [B_ctmp[tb]], bias=vcol(f"bpw2{jb}", d))
                stt(zt[:, d, :], xres[:, d, :], ALPHA, ctmp[tb], ALU.mult, ALU.add, [B_xres[d], B_ctmp[tb]], [B_z[d]])
                ln_feed(d)
        ln_finish(f"lnmg{li}", f"lnmb{li}", "resid")

    def bview(i):
        return abf[:, i * KC * T:(i + 1) * KC * T].rearrange("p (c t) -> p c t", t=T)

    qt, qS, kt, khtok, vtok, gs, og = (bview(i) for i in range(7))
    khT = [abf[:, 7 * KC * T + i * T:7 * KC * T + (i + 1) * T] for i in range(2)]
    B_qt, B_qS, B_kt, B_khtok, B_vtok, B_gs, B_og = (bl(n, KC) for n in ("qt", "qS", "kt", "khtok", "vtok", "gs", "og"))
    B_khT = bl("khT", 2)
    zbq = [zb[:].rearrange("p c t -> p (c t)").bitcast(F32), zsq[:].rearrange("p c t -> p (c t)").bitcast(F32)]
    hf32 = [[af32[:, (par * 7 + i) * T:(par * 7 + i + 1) * T] for i in range(7)] for par in range(2)]
    hf32.append([zbq[i // 4][:, (i % 4) * T:(i % 4 + 1) * T] for i in range(7)])
    B_hf = [bl(f"hf{par}_", 7) for par in range(2)]
    B_hf.append([(B_zb[2 * i], B_zb[2 * i + 1]) if i < 4 else (B_zsq[2 * (i - 4)], B_zsq[2 * (i - 4) + 1])
                 for i in range(7)])
    rs4 = [af32[:, g * 7 * T:g * 7 * T + 4 * T].rearrange("p (h t) -> p h t", t=T) for g in range(2)]
    at2 = [abf[:, 7 * KC * T + 2 * T + i * T:7 * KC * T + 2 * T + (i + 1) * T] for i in range(2)]
    B_at = bl("at", 2)

    def hgrn_mixer(li, ja, lbi):
        S.soft_barrier()
        rot = Rot([0, 1, 2, 3, 4, 5, 6, 7])
        lb_ap = lbv[:, 2 * lbi, :]
        oml_ap = lbv[:, 2 * lbi + 1, :]
        St, Sb = Sst[ja], Sbf[ja]
        def bufs(h):
            par = h % 3
            return hf32[par], B_hf[par]

        def s0(h):
            (A_, K_, Bc, D1, E1, D2, QS), (bA, bK, bB, bD1, bE1, bD2, bQS) = bufs(h)
            w, wb = wget()
            w4 = w[:].rearrange("p (k s n) -> p k s n", s=4, n=P)
            bf_, bfb = rot.next()
            bq, bqb = rot.next()
            bg, bgb = rot.next()
            if h == 0:
                for k in range(KC):
                    mm(bf_[:], w4[:, k, 1, :], xbf[:, k, :], k == 0, k == KC - 1, [wb, B_xbf[k]], [bfb])
                    mm(bq[:], w4[:, k, 0, :], xbf[:, k, :], k == 0, k == KC - 1, [wb, B_xbf[k]], [bqb])
                    mm(bg[:], w4[:, k, 3, :], xbf[:, k, :], k == 0, k == KC - 1, [wb, B_xbf[k]], [bgb])
            else:
                for k in range(KC):
                    mm(bf_[:], w4[:, k, 1, :], xbf[:, k, :], k == 0, k == KC - 1, [wb, B_xbf[k]], [bfb])
            act(A_, bf_[:], AF.Sigmoid, [bfb], [bA])
            if h != 0:
                for k in range(KC):
                    mm(bq[:], w4[:, k, 0, :], xbf[:, k, :], k == 0, k == KC - 1, [wb, B_xbf[k]], [bqb])
            act(QS, bq[:], AF.Sigmoid, [bqb], [bQS])
            tt("vector", QS, bq[:], QS, ALU.mult, [bqb, bQS], [bQS])
            if h != 0:
                for k in range(KC):
                    mm(bg[:], w4[:, k, 3, :], xbf[:, k, :], k == 0, k == KC - 1, [wb, B_xbf[k]], [bgb])
            act(E1, bg[:], AF.Sigmoid, [bgb], [bE1])
            tt("vector", gs[:, h, :], bg[:], E1, ALU.mult, [bgb, bE1], [B_gs[h]])
            bv, bvb = rot.next()
            for tc in range(4):
                for k in range(KC):
                    mm(bv[:, tc * P:(tc + 1) * P], xbf[:, k, tc * P:(tc + 1) * P], w4[:, k, 2, :], k == 0, k == KC - 1,
                       [wb, B_xbf[k]], [bvb])
            cp("vector", vtok[:, h, :], bv[:], [bvb], [B_vtok[h]])

        def s1a(h):
            (A_, K_, Bc, D1, E1, D2, QS), (bA, bK, bB, bD1, bE1, bD2, bQS) = bufs(h)
            ts("vector", A_, A_, oml_ap[:, h:h + 1], lb_ap[:, h:h + 1], ALU.mult, ALU.add, [bA, B_lb], [bA])
            ts("gpsimd", K_, A_, -1.0, 1.0, ALU.mult, ALU.add, [bA], [bK])
            ts("vector", A_, A_, GATE_EPS, None, ALU.max, None, [bA], [bA])
            act(A_, A_, AF.Ln, [bA], [bA])

        def s1b(h):
            (A_, K_, Bc, D1, E1, D2, QS), (bA, bK, bB, bD1, bE1, bD2, bQS) = bufs(h)
            S.add("vector", (lambda Bc, A_: lambda e: e.tensor_tensor_scan(out=Bc, data0=carry[:], data1=A_, initial=0.0,
                                                                            op0=ALU.mult, op1=ALU.add))(Bc, A_),
                  [bA, B_const], [bB])
            b3 = Bc.rearrange("p (c j) -> p c j", j=CH)
            tt("gpsimd", D1.rearrange("p (c j) -> p c j", j=CH), b3, b3[:, :, CH // 2 - 1:CH // 2].to_broadcast([P, NCH, CH]),
               ALU.subtract, [bB], [bD1])
            tt("gpsimd", D2.rearrange("p (c j) -> p c j", j=CH), b3, b3[:, :, CH - 1:CH].to_broadcast([P, NCH, CH]),
               ALU.subtract, [bB], [bD2])

        def s2a(h):
            (A_, K_, Bc, D1, E1, D2, QS), (bA, bK, bB, bD1, bE1, bD2, bQS) = bufs(h)
            act(E1, D1, AF.Exp, [bD1], [bE1])
            act(D1, D1, AF.Exp, [bD1], [bD1], scale=-1.0)
            act(D2, D2, AF.Exp, [bD2], [bD2], scale=-1.0)
            act(Bc, Bc, AF.Exp, [bB], [bB])

        def s2b(h):
            (A_, K_, Bc, D1, E1, D2, QS), (bA, bK, bB, bD1, bE1, bD2, bQS) = bufs(h)
            b3 = Bc.rearrange("p (c j) -> p c j", j=CH)
            tt("vector", qt[:, h, :], QS, E1, ALU.mult, [bQS, bE1], [B_qt[h]])
            tt("vector", kt[:, h, :], K_, D1, ALU.mult, [bK, bD1], [B_kt[h]])
            kp = h % 2
            tt("gpsimd", khT[kp], K_, D2, ALU.mult, [bK, bD2], [B_khT[kp]])
            cp("gpsimd", dec[:, h, :], b3[:, :, CH - 1:CH].rearrange("p c o -> p (c o)"), [bB], [B_dec[h]])
            tt("gpsimd", qS[:, h, :], QS, Bc, ALU.mult, [bQS, bB], [B_qS[h]])
            bk, bkb = rot.next()
            for tc in range(4):
                mm(bk[:, tc * P:(tc + 1) * P], khT[kp][:, tc * P:(tc + 1) * P], ident_b[:], True, True,
                   [B_khT[kp], B_const], [bkb])
            cp("vector", khtok[:, h, :], bk[:], [bkb], [B_khtok[h]])

        for step in range(KC + 2):
            h1, h2 = step - 1, step - 2
            if 0 <= h2 < KC:
                s2a(h2)
            if 0 <= h1 < KC:
                s1a(h1)
                s1b(h1)
            if step < KC:
                s0(step)
            if 0 <= h2 < KC:
                s2b(h2)
        for c in range(NCH):
            blk, hf = c // 2, c % 2
            r0, r1 = hf * CH, (hf + 1) * CH
            par = c % 2
            X, Xb = banks[3 + par], bank_buf[3 + par]
            Y, Yb = banks[5 + par], bank_buf[5 + par]
            Zs = [(banks[0], bank_buf[0]), (banks[1], bank_buf[1])] if par == 0 else \
                 [(banks[2], bank_buf[2]), (banks[7], bank_buf[7])]
            a_sb, a_b = at2[par], B_at[par]
            for h in range(KC):
                mm(X[:, h * CH:(h + 1) * CH], kt[:, h, blk * P:(blk + 1) * P], qt[:, h, c * CH:(c + 1) * CH], True, True,
                   [B_kt[h], B_qt[h]], [Xb])
            tt("vector", a_sb[r0:r1, :].rearrange("p (h t) -> p h t", t=CH),
               X[r0:r1, :].rearrange("p (h t) -> p h t", t=CH),
               cmask[r0:r1, 0:1, :].to_broadcast([CH, KC, CH]), ALU.mult, [Xb, B_const], [a_b])
            for h in range(KC):
                Z, Zb = Zs[h // 4]
                hh = h % 4
                mm(Z[:, hh * P:(hh + 1) * P], khtok[r0:r1, h, blk * P:(blk + 1) * P],
                   vtok[r0:r1, h, blk * P:(blk + 1) * P], True, True, [B_khtok[h], B_vtok[h]], [Zb])
            for h in range(KC):
                mm(Y[:, h * CH:(h + 1) * CH], vtok[r0:r1, h, blk * P:(blk + 1) * P], a_sb[r0:r1, h * CH:(h + 1) * CH],
                   True, False, [B_vtok[h], a_b], [Yb])
                mm(Y[:, h * CH:(h + 1) * CH], Sb[:, h, :], qS[:, h, c * CH:(c + 1) * CH], False, True,
                   [B_Sbg[ja][h // 4], B_qS[h]], [Yb])
            act(zt[:, :, c * CH:(c + 1) * CH], Y[:].rearrange("p (h t) -> p h t", t=CH), AF.Copy, [Yb], B_z)
            for g in range(2):
                Z, Zb = Zs[g]
                for hh in range(4):
                    h = 4 * g + hh
                    stt(St[:, h, :], St[:, h, :], dec[:, h, c:c + 1], Z[:, hh * P:(hh + 1) * P], ALU.mult, ALU.add,
                        [B_S[ja][h], B_dec[h], Zb], [B_S[ja][h]])
                act(Sb[:, 4 * g:4 * g + 4, :], St[:, 4 * g:4 * g + 4, :], AF.Copy, B_S[ja][4 * g:4 * g + 4], [B_Sbg[ja][g]])
        ango = voff[f"ang{ja}"]
        for g in range(2):
            hs = slice(4 * g, 4 * g + 4)
            for hh in range(4):
                h = 4 * g + hh
                act(zsq[:, h, :], zt[:, h, :], AF.Square, [B_z[h]], [B_zsq[h]])
                mm(banks[h][:], ones_h[:], zsq[:, h, :], True, True, [B_zsq[h], B_const], [bank_buf[h]])
            rb = B_hf[g][0:4]
            act(rs4[g], ps_all[:, hs, :], AF.Ln, bank_buf[hs] + [B_const], rb, bias=epsc[:, 1:2])
            act(rs4[g], rs4[g], AF.Exp, rb, rb, scale=-0.5)
            tt("vector", zt[:, hs, :], zt[:, hs, :], rs4[g], ALU.mult, B_z[hs] + rb, B_z[hs])
            for hh in range(4):
                h = 4 * g + hh
                stt(og[:, h, :], zt[:, h, :], vecs[:, ango + h:ango + h + 1], gs[:, h, :], ALU.mult, ALU.mult,
                    [B_z[h], B_gs[h], B_vecs], [B_og[h]])
        rot = Rot([0, 1, 2, 3, 4, 5])
        for s in range(2):
            w, wb = wget()
            w3 = w[:].rearrange("p (k n) -> p k n", n=512)
            grp = [(j,) + rot.next() for j in range(4)]
            if s == 0:
                for k in range(KC):
                    for j, bank, bbuf in grp:
                        mm(bank[:], w3[:, k, j * P:(j + 1) * P], og[:, k, :], k == 0, k == KC - 1, [wb, B_og[k]], [bbuf])
            for j, bank, bbuf in grp:
                d = 4 * s + j
                if s != 0:
                    for k in range(KC):
                        mm(bank[:], w3[:, k, j * P:(j + 1) * P], og[:, k, :], k == 0, k == KC - 1, [wb, B_og[k]], [bbuf])
                flush_stats()
                resid_from_psum(d, bank, bbuf)
        ln_finish(f"lnmg{li}", f"lnmb{li}", "resid")

    for lst in (B_hT, B_ftmp, B_u, B_cb, B_ctmp, B_qt, B_qS, B_kt, B_khtok, B_vtok, B_gs, B_og, B_khT,
                B_hf[0], B_hf[1], B_at):
        for b_ in lst:
            b_.arena = True

    xres_flat = xres[:].rearrange("p c t -> p (c t)")
    for it in range(n_tiles):
        for d in range(KC):
            src = zbq[d // 4][:, (d % 4) * T:(d % 4 + 1) * T]
            sbuf_ = (B_zb if d < 4 else B_zsq)[2 * (d % 4):2 * (d % 4) + 2]
            cp("vector" if d % 2 == 0 else "scalar", xbf[:, d, :], src, sbuf_, [B_xbf[d]])
        for d in range(KC):
            src = zbq[d // 4][:, (d % 4) * T:(d % 4 + 1) * T]
            sbuf_ = (B_zb if d < 4 else B_zsq)[2 * (d % 4):2 * (d % 4) + 2]
            cp("gpsimd" if d % 2 == 0 else "scalar", xres[:, d, :], src, sbuf_, [B_xres[d]])
        ja = jb = 0
        for li, k in enumerate(kinds):
            if it == 0:
                for pi, l2 in piece_layer:
                    if l2 == li + 1:
                        issue_cast(pi)
            if k == "A":
                hgrn_mixer(li, ja, ja)
                ja += 1
            else:
                conv_mixer(li, jb)
                jb += 1
            if li == L - 1 and it + 1 < n_tiles:
                ffn(li, hook=(lambda it: lambda: issue_xload(it + 1))(it))
            else:
                ffn(li)
        S.add("gpsimd", (lambda it: lambda e: e.dma_start(out=out_d[it], in_=xres_flat))(it), B_xres, (), dma_key="xout")
    S.add("gpsimd", lambda e: None, (), (), extra=[S.dma_last["xout"]])
    S.finalize()

    eng_sems = {e: es.enter_context(nc.semaphore(f"sem_{e}")) for e in ENGS}
    dma_sems = {k: es.enter_context(nc.semaphore(f"dsem_{k}")) for k in S.dma_count}
    with nc.Block() as block:
        @block.sync
        def _(e):
            S.emit("sync", e, eng_sems, dma_sems)

        @block.tensor
        def _(e):
            S.emit("tensor", e, eng_sems, dma_sems)

        @block.vector
        def _(e):
            S.emit("vector", e, eng_sems, dma_sems)

        @block.scalar
        def _(e):
            S.emit("scalar", e, eng_sems, dma_sems)

        @block.gpsimd
        def _(e):
            S.emit("gpsimd", e, eng_sems, dma_sems)
    es.close()
    return nc, S


def x_to_fm(xb, n_tiles):
    a = xb.reshape(n_tiles, T, KC, P).transpose(0, 3, 2, 1)
    return np.ascontiguousarray(a).reshape(n_tiles, P, KC * T)


def fm_to_x(o, n_tiles):
    a = o.reshape(n_tiles, P, KC, T).transpose(0, 3, 2, 1)
    return np.ascontiguousarray(a).reshape(n_tiles * T, D)


def run(x, inp, kinds, layer_ids, core_ids, placement=None):
    x = np.asarray(x, np.float32)
    nb, seqlen, _ = x.shape
    n_tiles = seqlen // T
    wslabs, vecs, seq, nh = pack_weights(kinds, layer_ids, inp)
    nc, _ = build(n_tiles, kinds)
    if placement is None:
        placement = list(range(nb))
    ncore = len(core_ids)
    in_maps = [None] * ncore
    for b, c in enumerate(placement):
        in_maps[c] = {"x_fm": x_to_fm(x[b], n_tiles), "wslabs": wslabs, "vecs": vecs}
    idle = None
    for c in range(ncore):
        if in_maps[c] is None:
            if idle is None:
                idle = {"x_fm": np.zeros((n_tiles, P, KC * T), np.float32), "wslabs": np.zeros_like(wslabs),
                        "vecs": np.zeros_like(vecs)}
            in_maps[c] = idle
    res = run_bass_kernel_spmd(nc, in_maps, core_ids=core_ids)
    return np.stack([fm_to_x(np.asarray(res.results[c]["out_fm"]), n_tiles) for c in placement])


def kernel(**inputs):
    x = np.asarray(inputs["x"], np.float32)
    kinds = ["A", "B", "A", "B"]
    out = run(x, inputs, kinds, [0, 1, 2, 3], list(range(x.shape[0])))
    return out.astype(np.float32)
```

```python
import numpy as np
from contextlib import ExitStack
import concourse.bass as bass
import concourse.mybir as mybir
from concourse.bass_utils import run_bass_kernel_spmd

F32 = mybir.dt.float32
BF16 = mybir.dt.bfloat16
AF = mybir.ActivationFunctionType
ALU = mybir.AluOpType

P = 128
D = 1024
KC = 8
DFF = 4096
T = 512
CH = 64
NCH = T // CH
KW = 31
SLAB = 4096
ALPHA = 8.0 ** 0.25
LN_EPS = 1e-5
RMS_EPS = 1e-6
GATE_EPS = 1e-6
NSLOT = 4
N_CORES = 4


class Buf:
    __slots__ = ("name", "w", "r", "arena")

    def __init__(self, name):
        self.name = name
        self.w = None
        self.r = []
        self.arena = False


def _flat(xs):
    out = []
    for b in xs:
        if isinstance(b, (tuple, list)):
            out.extend(_flat(b))
        else:
            out.append(b)
    return out


class Op:
    __slots__ = ("eng", "fn", "deps", "idx", "sig", "sigval", "is_dma", "key", "dma_val")


COMPUTE = ("tensor", "vector", "scalar", "gpsimd")
ENGS = ("sync",) + COMPUTE


class Sched:
    def __init__(self):
        self.ops = {e: [] for e in ENGS}
        self.last = {e: None for e in ENGS}
        self.pending = {e: set() for e in ENGS}
        self.dma_count = {}
        self.dma_last = {}
        self.arena_last = {e: None for e in ENGS}
        self.arena_pending = {e: set() for e in ENGS}

    def soft_barrier(self):
        snap = [self.arena_last[e] for e in COMPUTE if self.arena_last[e] is not None]
        for e in COMPUTE:
            self.arena_pending[e].update(snap)

    def add(self, eng, fn, reads=(), writes=(), dma_key=None, extra=()):
        reads = _flat(reads)
        writes = _flat(writes)
        op = Op()
        op.eng = eng
        op.fn = fn
        op.sig = False
        op.sigval = 0
        op.is_dma = dma_key is not None
        op.key = dma_key
        deps = set(extra)
        for b in reads:
            if b.w is not None:
                deps.add(b.w)
        for b in writes:
            if b.w is not None:
                deps.add(b.w)
            deps.update(b.r)
        for b in writes:
            b.w = op
            b.r = []
        for b in reads:
            b.r.append(op)
        if self.pending[eng]:
            deps.update(self.pending[eng])
            self.pending[eng] = set()
        if any(b.arena for b in reads) or any(b.arena for b in writes):
            if self.arena_pending[eng]:
                deps.update(self.arena_pending[eng])
                self.arena_pending[eng] = set()
            self.arena_last[eng] = op
        if op.is_dma:
            prev = self.dma_last.get(dma_key)
            if prev is not None:
                deps.add(prev)
            self.dma_count[dma_key] = self.dma_count.get(dma_key, 0) + 16
            op.dma_val = self.dma_count[dma_key]
            self.dma_last[dma_key] = op
        deps.discard(op)
        op.deps = deps
        op.idx = len(self.ops[eng])
        self.ops[eng].append(op)
        self.last[eng] = op
        return op

    def barrier(self):
        snap = [self.last[e] for e in COMPUTE if self.last[e] is not None]
        for e in COMPUTE:
            self.pending[e].update(o for o in snap if o.eng != e)

    def finalize(self):
        for e in ENGS:
            for op in self.ops[e]:
                for d in op.deps:
                    if not d.is_dma and (d.eng != e or e != "tensor"):
                        d.sig = True
        for e in ENGS:
            n = 0
            for op in self.ops[e]:
                if op.sig and not op.is_dma:
                    n += 1
                    op.sigval = n

    def emit(self, eng_name, eng, eng_sems, dma_sems):
        waited = {}
        for op in self.ops[eng_name]:
            need = {}
            for d in op.deps:
                if d.is_dma:
                    k = ("dma", d.key)
                    v = d.dma_val
                elif d.eng == eng_name and eng_name == "tensor":
                    continue
                else:
                    k = d.eng
                    v = d.sigval
                if need.get(k, 0) < v:
                    need[k] = v
            for k, v in need.items():
                if waited.get(k, 0) >= v:
                    continue
                waited[k] = v
                sem = dma_sems[k[1]] if isinstance(k, tuple) else eng_sems[k]
                eng.wait_ge(sem, v)
            ins = op.fn(eng)
            if ins is None:
                continue
            if op.is_dma:
                ins.then_inc(dma_sems[op.key], 16)
            elif op.sig:
                ins.then_inc(eng_sems[eng_name], 1)


def vec_layout(kinds):
    off = {}
    n = 0

    def put(name, cols):
        nonlocal n
        off[name] = n
        n += cols

    for i in range(len(kinds)):
        for nm in ("lnmg", "lnmb", "lnfg", "lnfb"):
            put(f"{nm}{i}", KC)
    put("lbl", 2 * KC)
    ja = jb = 0
    for k in kinds:
        if k == "A":
            put(f"ang{ja}", KC)
            ja += 1
        else:
            put(f"bpw1{jb}", 2 * KC)
            for nm in ("bdw", "blg", "blb", "bpw2"):
                put(f"{nm}{jb}", KC)
            put(f"wdw{jb}", KC * KW)
            jb += 1
    return off, n


def fm_vec(v):
    v = np.asarray(v, np.float32)
    return np.ascontiguousarray(v.reshape(-1, P).T)


def colslabs(w, ns):
    K, N = w.shape
    kc = K // P
    a = w.reshape(kc, P, N // ns, ns).transpose(2, 1, 0, 3)
    return np.ascontiguousarray(a).reshape(N // ns, P, kc * ns)


def pack_weights(kinds, layer_ids, inp):
    slabs = []
    seq = []
    conv_ids = {}
    off, nv = vec_layout(kinds)
    vecs = np.zeros((P, nv), np.float32)
    lbl = np.asarray(inp["a_lb_logits"], np.float32)
    vecs[:, off["lbl"]:off["lbl"] + 2 * KC] = fm_vec(lbl.reshape(-1))
    n_conv = 0
    pending_conv = []
    ja = jb = 0
    for li, k in enumerate(kinds):
        i = layer_ids[li]
        j = i // 2
        vecs[:, off[f"lnmg{li}"]:off[f"lnmg{li}"] + KC] = fm_vec(inp["ln_mix_g"][i])
        vecs[:, off[f"lnmb{li}"]:off[f"lnmb{li}"] + KC] = fm_vec(inp["ln_mix_b"][i])
        vecs[:, off[f"lnfg{li}"]:off[f"lnfg{li}"] + KC] = fm_vec(inp["ln_ffn_g"][i])
        vecs[:, off[f"lnfb{li}"]:off[f"lnfb{li}"] + KC] = fm_vec(inp["ln_ffn_b"][i])
        if k == "A":
            w_in = np.asarray(inp["a_w_in"][j], np.float32)
            a = w_in.reshape(KC, P, 4, KC, P).transpose(3, 1, 0, 2, 4)
            a = np.ascontiguousarray(a).reshape(KC, P, SLAB)
            for h in range(KC):
                seq.append(len(slabs))
                slabs.append(a[h])
            for s in colslabs(np.asarray(inp["a_w_out"][j], np.float32), 512):
                seq.append(len(slabs))
                slabs.append(s)
            vecs[:, off[f"ang{ja}"]:off[f"ang{ja}"] + KC] = fm_vec(inp["a_norm_g"][j])
            ja += 1
        else:
            w1 = np.asarray(inp["b_w_pw1"][j], np.float32)
            a = w1.reshape(KC, P, 2, 4, 2, P).transpose(3, 1, 0, 2, 4, 5)
            a = np.ascontiguousarray(a).reshape(4, P, SLAB)
            for s in range(4):
                seq.append(len(slabs))
                slabs.append(a[s])
            for c in range(KC):
                seq.append(("conv", jb, c))
            for s in colslabs(np.asarray(inp["b_w_pw2"][j], np.float32), 512):
                seq.append(len(slabs))
                slabs.append(s)
            vecs[:, off[f"bpw1{jb}"]:off[f"bpw1{jb}"] + 2 * KC] = fm_vec(inp["b_b_pw1"][j])
            vecs[:, off[f"bdw{jb}"]:off[f"bdw{jb}"] + KC] = fm_vec(inp["b_b_dw"][j])
            vecs[:, off[f"blg{jb}"]:off[f"blg{jb}"] + KC] = fm_vec(inp["b_ln_g"][j])
            vecs[:, off[f"blb{jb}"]:off[f"blb{jb}"] + KC] = fm_vec(inp["b_ln_b"][j])
            vecs[:, off[f"bpw2{jb}"]:off[f"bpw2{jb}"] + KC] = fm_vec(inp["b_b_pw2"][j])
            wdw = np.asarray(inp["b_w_dw"][j], np.float32)
            vecs[:, off[f"wdw{jb}"]:off[f"wdw{jb}"] + KC * KW] = np.ascontiguousarray(
                wdw.reshape(KW, KC, P).transpose(2, 1, 0)).reshape(P, KC * KW)
            jb += 1
        for s in colslabs(np.asarray(inp["ffn_w1"][i], np.float32), 512):
            seq.append(len(slabs))
            slabs.append(s)
        for s in colslabs(np.asarray(inp["ffn_w2"][i], np.float32), 128):
            seq.append(len(slabs))
            slabs.append(s)
    nh = len(slabs)
    seq2 = []
    for s in seq:
        if isinstance(s, tuple):
            seq2.append(nh + s[1] * KC + s[2])
        else:
            seq2.append(s)
    return np.stack(slabs), vecs, seq2, nh


def slab_seq(kinds):
    seq = []
    n = 0
    jb = 0
    for k in kinds:
        if k == "A":
            cnt = [8, 2]
        else:
            cnt = [4, ("conv", jb), 2]
            jb += 1
        for c in cnt + [8, 8]:
            if isinstance(c, tuple):
                seq.extend(("conv", c[1], x) for x in range(KC))
            else:
                seq.extend(range(n, n + c))
                n += c
    nh = n
    return [nh + s[1] * KC + s[2] if isinstance(s, tuple) else s for s in seq], nh


def build(n_tiles, kinds):
    nc = bass.Bass("TRN2", target_bir_lowering=False)
    L = len(kinds)
    nA = sum(1 for k in kinds if k == "A")
    nB = L - nA
    voff, NV = vec_layout(kinds)
    seq, NH = slab_seq(kinds)
    NS = NH + nB * KC

    x_in = nc.dram_tensor("x_fm", [n_tiles, P, KC * T], F32, kind="ExternalInput").ap()
    w_in_d = nc.dram_tensor("wslabs", [NH, P, SLAB], F32, kind="ExternalInput").ap()
    vecs_d = nc.dram_tensor("vecs", [P, NV], F32, kind="ExternalInput").ap()
    out_d = nc.dram_tensor("out_fm", [n_tiles, P, KC * T], F32, kind="ExternalOutput").ap()
    wsc = nc.dram_tensor("wsc", [NS, P, SLAB], BF16).ap()

    S = Sched()
    es = ExitStack()

    def sb(name, shape, dt):
        return es.enter_context(nc.sbuf_tensor(name, shape, dt))

    xres = sb("xres", [P, KC, T], F32)
    xbf = sb("xbf", [P, KC, T], BF16)
    zt = sb("zt", [P, KC, T], F32)
    zb = sb("zb", [P, KC, T], BF16)
    zsq = sb("zsq", [P, KC, T], BF16)
    st_mean = sb("st_mean", [P, T], F32)
    st_tmp = sb("st_tmp", [P, T], F32)
    st_rstd = sb("st_rstd", [P, T], F32)
    wslot = [sb(f"wslot{i}", [P, SLAB], BF16) for i in range(NSLOT)]
    vecs = sb("vecs_sb", [P, NV], F32)
    lbv = sb("lbv", [P, 4, KC], F32)
    ident_f = sb("ident_f", [P, P], F32)
    ident_b = sb("ident_b", [P, P], BF16)
    ones_d = sb("ones_d", [P, P], BF16)
    ones_h = sb("ones_h", [P, P], BF16)
    cmask = sb("cmask", [P, 1, CH], F32)
    epsc = sb("epsc", [P, 2], F32)
    carry = sb("carry", [P, T], F32)
    Sst = [sb(f"Sst{j}", [P, KC, P], F32) for j in range(max(nA, 1))]
    Sbf = [sb(f"Sbf{j}", [P, KC, P], BF16) for j in range(max(nA, 1))]
    halo = [sb(f"halo{j}", [P, KC, KW - 1], BF16) for j in range(max(nB, 1))]
    dec = sb("dec", [P, KC, NCH], F32)
    NBF = 30720
    NF32 = 7168
    abf = sb("abf", [P, NBF], BF16)
    af32 = sb("af32", [P, NF32], F32)

    ps_all = es.enter_context(nc.psum_tensor("ps_all", [P, 8, T], F32))
    banks = [ps_all[:, i, :] for i in range(8)]
    bank_buf = [Buf(f"bank{i}") for i in range(8)]

    class Rot:
        def __init__(self, ids):
            self.ids = ids
            self.i = 0

        def next(self):
            b = self.ids[self.i % len(self.ids)]
            self.i += 1
            return banks[b], bank_buf[b]

    def bl(name, n):
        return [Buf(f"{name}{i}") for i in range(n)]

    B_xres = bl("xres", KC)
    B_xbf = bl("xbf", KC)
    B_z = bl("z", KC)
    B_zb = bl("zb", KC)
    B_zsq = bl("zsq", KC)
    B_mean = Buf("mean")
    B_tmp = Buf("sttmp")
    B_rstd = Buf("rstd")
    B_slot = bl("slot", NSLOT)
    B_vecs = Buf("vecs")
    B_const = Buf("const")
    B_cast = bl("cast", L)
    B_conv = {}
    B_S = [bl(f"S{j}_", KC) for j in range(max(nA, 1))]
    B_Sbg = [bl(f"Sbf{j}_", 2) for j in range(max(nA, 1))]
    B_halo = bl("halo", max(nB, 1))
    B_dec = bl("dec", KC)

    def mm(out, lhsT, rhs, start, stop, reads, writes):
        return S.add("tensor", lambda e: e.matmul(out, lhsT=lhsT, rhs=rhs, start=start, stop=stop), reads, writes)

    def act(out, in_, func, reads, writes, bias=None, scale=None):
        kw = {}
        if bias is not None:
            kw["bias"] = bias
        if scale is not None:
            kw["scale"] = scale
        return S.add("scalar", lambda e: e.activation(out=out, in_=in_, func=func, **kw), reads, writes)

    def tt(eng, out, in0, in1, op, reads, writes):
        return S.add(eng, lambda e: e.tensor_tensor(out=out, in0=in0, in1=in1, op=op), reads, writes)

    def ts(eng, out, in0, s1, s2, op0, op1, reads, writes):
        if op1 is None:
            return S.add(eng, lambda e: e.tensor_scalar(out, in0, s1, None, op0), reads, writes)
        return S.add(eng, lambda e: e.tensor_scalar(out, in0, s1, s2, op0, op1), reads, writes)

    def stt(out, in0, scalar, in1, op0, op1, reads, writes):
        return S.add("vector", lambda e: e.scalar_tensor_tensor(out=out, in0=in0, scalar=scalar, in1=in1,
                                                                 op0=op0, op1=op1), reads, writes)

    def cp(eng, out, in_, reads, writes):
        if eng == "scalar":
            return act(out, in_, AF.Copy, reads, writes)
        return S.add(eng, lambda e: e.tensor_copy(out=out, in_=in_), reads, writes)

    def rsqrt(out, in_, epsi, reads, writes):
        act(out, in_, AF.Ln, list(reads) + [B_const], writes, bias=epsc[:, epsi:epsi + 1])
        act(out, out, AF.Exp, writes, writes, scale=-0.5)

    def memset(eng, ap, val, writes):
        return S.add(eng, lambda e: e.memset(ap, val), (), writes)

    def vcol(name, c=0):
        o = voff[name] + c
        return vecs[:, o:o + 1]

    wstate = {"issued": 0, "taken": 0}
    total_loads = n_tiles * len(seq)

    def slab_src_buf(sidx):
        if sidx >= NH:
            return B_conv[sidx]
        return B_piece[slab_piece[sidx]]

    slab_layer = {}
    _n = 0
    for li, k in enumerate(kinds):
        cnt = (8 + 2 if k == "A" else 4 + 2) + 16
        for s in range(_n, _n + cnt):
            slab_layer[s] = li
        _n += cnt
    assert _n == NH

    def issue_load(j):
        sidx = seq[j % len(seq)]
        slot = j % NSLOT
        if sidx >= NH:
            dst = wslot[slot][:, 0:KW * P]
            src = wsc[sidx][:, 0:KW * P]
        else:
            dst = wslot[slot][:]
            src = wsc[sidx]
        S.add("sync", lambda e: e.dma_start(out=dst, in_=src), [slab_src_buf(sidx)], [B_slot[slot]],
              dma_key=f"slot{slot}")

    def wget(la=NSLOT):
        i = wstate["taken"]
        while wstate["issued"] < min(i + la, total_loads):
            issue_load(wstate["issued"])
            wstate["issued"] += 1
        wstate["taken"] += 1
        return wslot[i % NSLOT], B_slot[i % NSLOT]

    S.add("gpsimd", lambda e: e.dma_start(out=vecs[:], in_=vecs_d), (), [B_vecs], dma_key="vecs")
    HT = KC * T // 2
    zbq = [zb[:].rearrange("p c t -> p (c t)").bitcast(F32), zsq[:].rearrange("p c t -> p (c t)").bitcast(F32)]

    def issue_xload(it):
        S.add("gpsimd", (lambda it: lambda e: e.dma_start(out=zbq[0], in_=x_in[it][:, 0:HT]))(it), (), B_zb, dma_key="xin0")
        S.add("gpsimd", (lambda it: lambda e: e.dma_start(out=zbq[1], in_=x_in[it][:, HT:2 * HT]))(it), (), B_zsq,
              dma_key="xin1")

    issue_xload(0)
    pieces = []
    lo = 0
    for li, k in enumerate(kinds):
        cnt = (8 + 2 if k == "A" else 4 + 2) + 16
        if li == 0:
            cuts = [lo, lo + 1, lo + 4, lo + 10, lo + cnt]
        else:
            cuts = [lo, lo + cnt]
        for a, b in zip(cuts[:-1], cuts[1:]):
            pieces.append((a, b))
        lo += cnt
    B_piece = bl("castp", len(pieces))
    slab_piece = {}
    piece_layer = []
    lo_ = 0
    for li, k in enumerate(kinds):
        cnt = (8 + 2 if k == "A" else 4 + 2) + 16
        for pi, (a, b) in enumerate(pieces):
            if lo_ <= a < lo_ + cnt:
                piece_layer.append((pi, li))
        lo_ += cnt

    def issue_cast(pi):
        a, b = pieces[pi]
        for sidx in range(a, b):
            slab_piece[sidx] = pi
        S.add("gpsimd", (lambda a, b: lambda e: e.dma_start(
            out=wsc[a:b].rearrange("s p n -> (s p) n"), in_=w_in_d[a:b].rearrange("s p n -> (s p) n")))(a, b),
            (), [B_piece[pi]], dma_key=f"cast{pi}")

    for pi, li in piece_layer:
        if li == 0:
            issue_cast(pi)
    memset("vector", ident_f[:], 1.0, [B_const])
    S.add("gpsimd", lambda e: e.affine_select(out=ident_f[:], in_=ident_f[:], pattern=[[-1, P]],
                                              compare_op=ALU.is_equal, fill=0.0, base=0, channel_multiplier=1),
          [B_const], [B_const])
    cp("vector", ident_b[:], ident_f[:], [B_const], [B_const])
    memset("vector", ones_d[:], 1.0 / D, [B_const])
    memset("vector", ones_h[:], 1.0 / P, [B_const])
    memset("vector", cmask[:], 1.0, [B_const])
    for hf in range(2):
        S.add("gpsimd", (lambda hf: lambda e: e.affine_select(
            out=cmask[hf * CH:(hf + 1) * CH, 0, :], in_=cmask[hf * CH:(hf + 1) * CH, 0, :], pattern=[[1, CH]],
            compare_op=ALU.is_ge, fill=0.0, base=0, channel_multiplier=-1))(hf), [B_const], [B_const])
    memset("vector", epsc[:, 0:1], LN_EPS, [B_const])
    memset("vector", epsc[:, 1:2], RMS_EPS, [B_const])
    memset("vector", carry[:], 1.0, [B_const])
    memset("vector", carry[:].rearrange("p (c j) -> p c j", j=CH)[:, :, 0:1], 0.0, [B_const])
    for j in range(max(nA, 1)):
        memset("vector", Sst[j][:], 0.0, B_S[j])
        memset("gpsimd", Sbf[j][:], 0.0, B_Sbg[j])
    for j in range(max(nB, 1)):
        memset("gpsimd", halo[j][:], 0.0, [B_halo[j]])
    B_lb = Buf("lb")
    l0 = vecs[:, voff["lbl"]:voff["lbl"] + KC]
    l1 = vecs[:, voff["lbl"] + KC:voff["lbl"] + 2 * KC]
    tt("vector", lbv[:, 1, :], l1, l0, ALU.subtract, [B_vecs], [B_lb])
    act(lbv[:, 2, :], lbv[:, 1, :], AF.Sigmoid, [B_lb], [B_lb])
    act(lbv[:, 3, :], lbv[:, 1, :], AF.Sigmoid, [B_lb], [B_lb], scale=-1.0)
    tt("vector", lbv[:, 0, :], lbv[:, 3, :], lbv[:, 3, :], ALU.subtract, [B_lb], [B_lb])
    ts("vector", lbv[:, 1, :], lbv[:, 0, :], -1.0, 1.0, ALU.mult, ALU.add, [B_lb], [B_lb])
    ts("vector", lbv[:, 3, :], lbv[:, 2, :], -1.0, 1.0, ALU.mult, ALU.add, [B_lb], [B_lb])
    dgs = sb("dgs", [P, 16 * P], BF16)
    B_dgs = Buf("dgs")
    diag_jobs = []
    for jb in range(nB):
        for c in range(KC):
            sidx = NH + jb * KC + c
            B_conv[sidx] = Buf(f"conv{sidx}")
            for half in range(2):
                def job(jb=jb, c=c, sidx=sidx, half=half):
                    k0 = 16 * half
                    nk = 16 if half == 0 else KW - 16
                    o = voff[f"wdw{jb}"] + c * KW + k0
                    tt("vector", dgs[:, 0:nk * P].rearrange("p (k m) -> p k m", m=P),
                       ident_f[:].unsqueeze(1).to_broadcast([P, nk, P]),
                       vecs[:, o:o + nk].unsqueeze(2).to_broadcast([P, nk, P]), ALU.mult,
                       [B_vecs, B_const], [B_dgs])
                    S.add("scalar", lambda e: e.dma_start(out=wsc[sidx][:, k0 * P:(k0 + nk) * P], in_=dgs[:, 0:nk * P]),
                          [B_dgs], [B_conv[sidx]], dma_key="dgs")
                diag_jobs.append(job)
    S.barrier()

    stat1 = (banks[6], bank_buf[6])
    stat2 = (banks[7], bank_buf[7])

    pend = []

    def ln_feed(d):
        act(zb[:, d, :], zt[:, d, :], AF.Copy, [B_z[d]], [B_zb[d]])
        act(zsq[:, d, :], zt[:, d, :], AF.Square, [B_z[d]], [B_zsq[d]])
        pend.append(d)

    def flush_stats():
        for d in pend:
            mm(stat1[0][:], ones_d[:], zb[:, d, :], d == 0, d == KC - 1, [B_zb[d], B_const], [stat1[1]])
            mm(stat2[0][:], ones_d[:], zsq[:, d, :], d == 0, d == KC - 1, [B_zsq[d], B_const], [stat2[1]])
        pend.clear()

    def ln_finish(gname, bname, mode, dst=None, B_dst=None, hook=None):
        flush_stats()
        if hook is not None:
            hook()
        act(st_tmp[:], stat1[0][:], AF.Square, [stat1[1]], [B_tmp])
        tt("vector", st_tmp[:], stat2[0][:], st_tmp[:], ALU.subtract, [stat2[1], B_tmp], [B_tmp])
        rsqrt(st_rstd[:], st_tmp[:], 0, [B_tmp], [B_rstd])
        for dp in range(KC // 2):
            ds_ = slice(2 * dp, 2 * dp + 2)
            tt("vector", zt[:, ds_, :], zt[:, ds_, :], stat1[0].unsqueeze(1).to_broadcast([P, 2, T]), ALU.subtract,
               B_z[ds_] + [stat1[1]], B_z[ds_])
            tt("vector", zt[:, ds_, :], zt[:, ds_, :], st_rstd[:].unsqueeze(1).to_broadcast([P, 2, T]), ALU.mult,
               B_z[ds_] + [B_rstd], B_z[ds_])
        order = list(range(KC))
        for d in order:
            if mode == "resid":
                act(xbf[:, d, :], zt[:, d, :], AF.Identity, [B_z[d], B_vecs], [B_xbf[d]],
                    bias=vcol(bname, d), scale=vcol(gname, d))
            else:
                act(dst[:, d, :], zt[:, d, :], AF.Silu, [B_z[d], B_vecs], [B_dst[d]],
                    bias=vcol(bname, d), scale=vcol(gname, d))
        if mode == "resid":
            for d in order:
                if d % 2 == 0:
                    act(xres[:, d, :], zt[:, d, :], AF.Identity, [B_z[d], B_vecs], [B_xres[d]],
                        bias=vcol(bname, d), scale=vcol(gname, d))
                else:
                    ts("vector", xres[:, d, :], zt[:, d, :], vcol(gname, d), vcol(bname, d), ALU.mult, ALU.add,
                       [B_z[d], B_vecs], [B_xres[d]])

    def resid_from_psum(d, bank, bbuf):
        stt(zt[:, d, :], xres[:, d, :], ALPHA, bank[:], ALU.mult, ALU.add, [B_xres[d], bbuf], [B_z[d]])
        ln_feed(d)

    hT = abf[:, 0:32 * T].rearrange("p (c t) -> p c t", t=T)
    B_hT = bl("hT", 32)
    ftmp = [af32[:, i * T:(i + 1) * T] for i in range(3)]
    B_ftmp = bl("ftmp", 3)

    def ffn(li, hook=None):
        S.soft_barrier()
        rot = Rot([0, 1, 2, 3, 4, 5])
        n = 0
        NW = 6
        slabs = [wget(), wget(la=NSLOT - 1)]

        def evac(f, bank, bbuf):
            nonlocal n
            tb = n % 3
            act(ftmp[tb], bank[:], AF.Relu, [bbuf], [B_ftmp[tb]])
            tt("vector", hT[:, f, :], ftmp[tb], ftmp[tb], ALU.mult,
               [B_ftmp[tb]], [B_hT[f]])
            n += 1
            if diag_jobs:
                diag_jobs.pop(0)()

        wave = []
        for f in range(NW):
            w, wb = slabs[f // 4]
            w3 = w[:].rearrange("p (k n) -> p k n", n=512)
            bank, bbuf = rot.next()
            wave.append((f, w3, wb, bank, bbuf))
        for k in range(KC):
            for f, w3, wb, bank, bbuf in wave:
                j = f % 4
                mm(bank[:], w3[:, k, j * P:(j + 1) * P], xbf[:, k, :], k == 0, k == KC - 1, [wb, B_xbf[k]], [bbuf])
        for f, w3, wb, bank, bbuf in wave:
            evac(f, bank, bbuf)
        for s in range(1, 8):
            w, wb = slabs[s] if s < 2 else wget()
            w3 = w[:].rearrange("p (k n) -> p k n", n=512)
            for j in range(4):
                f = 4 * s + j
                if f < NW:
                    continue
                bank, bbuf = rot.next()
                for k in range(KC):
                    mm(bank[:], w3[:, k, j * P:(j + 1) * P], xbf[:, k, :], k == 0, k == KC - 1, [wb, B_xbf[k]], [bbuf])
                evac(f, bank, bbuf)
        for d in range(KC):
            w, wb = wget()
            w3 = w[:].rearrange("p (k n) -> p k n", n=P)
            bank, bbuf = rot.next()
            for f in range(32):
                mm(bank[:], w3[:, f, :], hT[:, f, :], f == 0, f == 31, [wb, B_hT[f]], [bbuf])
            flush_stats()
            resid_from_psum(d, bank, bbuf)
        ln_finish(f"lnfg{li}", f"lnfb{li}", "resid", hook=hook)

    UW = T + KW - 1
    ubuf = abf[:, 0:KC * UW].rearrange("p (c t) -> p c t", t=UW)
    cbuf = abf[:, KC * UW:KC * UW + KC * T].rearrange("p (c t) -> p c t", t=T)
    B_u = bl("u", KC)
    B_cb = bl("cb", KC)
    ctmp = [af32[:, i * T:(i + 1) * T] for i in range(3)]
    B_ctmp = bl("ctmp", 3)

    def conv_mixer(li, jb):
        while diag_jobs:
            diag_jobs.pop(0)()
        S.soft_barrier()
        rot = Rot([0, 1, 2, 3, 4, 5])
        cp("gpsimd", ubuf[:, :, 0:KW - 1], halo[jb][:], [B_halo[jb]], B_u)
        n = 0
        for s in range(4):
            w, wb = wget()
            w4 = w[:].rearrange("p (k q n) -> p k q n", q=4, n=P)
            grp = []
            for j in range(2):
                ba, bab = rot.next()
                bg, bgb = rot.next()
                grp.append((j, ba, bab, bg, bgb))
            if s == 0:
                for k in range(KC):
                    for j, ba, bab, bg, bgb in grp:
                        mm(ba[:], w4[:, k, j, :], xbf[:, k, :], k == 0, k == KC - 1, [wb, B_xbf[k]], [bab])
                        mm(bg[:], w4[:, k, 2 + j, :], xbf[:, k, :], k == 0, k == KC - 1, [wb, B_xbf[k]], [bgb])
            for j, ba, bab, bg, bgb in grp:
                c = 2 * s + j
                if s != 0:
                    for k in range(KC):
                        mm(ba[:], w4[:, k, j, :], xbf[:, k, :], k == 0, k == KC - 1, [wb, B_xbf[k]], [bab])
                    for k in range(KC):
                        mm(bg[:], w4[:, k, 2 + j, :], xbf[:, k, :], k == 0, k == KC - 1, [wb, B_xbf[k]], [bgb])
                tb = n % 3
                n += 1
                act(ctmp[tb], bg[:], AF.Sigmoid, [bgb, B_vecs], [B_ctmp[tb]], bias=vcol(f"bpw1{jb}", KC + c))
                stt(ubuf[:, c, KW - 1:UW], ba[:], vcol(f"bpw1{jb}", c), ctmp[tb], ALU.add, ALU.mult,
                    [bab, B_ctmp[tb], B_vecs], [B_u[c]])
        for c in range(KC):
            w, wb = wget()
            w3 = w[:, 0:KW * P].rearrange("p (k n) -> p k n", n=P)
            bank, bbuf = rot.next()
            for k in range(KW):
                mm(bank[:], w3[:, k, :], ubuf[:, c, k:k + T], k == 0, k == KW - 1, [wb, B_u[c]], [bbuf])
            flush_stats()
            act(zt[:, c, :], bank[:], AF.Identity, [bbuf, B_vecs], [B_z[c]], bias=vcol(f"bdw{jb}", c))
            ln_feed(c)
        cp("gpsimd", halo[jb][:], ubuf[:, :, T:UW], B_u, [B_halo[jb]])
        ln_finish(f"blg{jb}", f"blb{jb}", "conv", cbuf, B_cb)
        n = 0
        for s in range(2):
            w, wb = wget()
            w3 = w[:].rearrange("p (k n) -> p k n", n=512)
            grp = [(j,) + rot.next() for j in range(4)]
            if s == 0:
                for k in range(KC):
                    for j, bank, bbuf in grp:
                        mm(bank[:], w3[:, k, j * P:(j + 1) * P], cbuf[:, k, :], k == 0, k == KC - 1, [wb, B_cb[k]], [bbuf])
            for j, bank, bbuf in grp:
                d = 4 * s + j
                if s != 0:
                    for k in range(KC):
                        mm(bank[:], w3[:, k, j * P:(j + 1) * P], cbuf[:, k, :], k == 0, k == KC - 1, [wb, B_cb[k]], [bbuf])
                flush_stats()
                tb = n % 3
                n += 1
                act(ctmp[tb], bank[:], AF.Identity, [bbuf, B_vecs], [B_ctmp[tb]], bias=vcol(f"bpw2{jb}", d))
                stt(zt[:, d, :], xres[:, d, :], ALPHA, ctmp[tb], ALU.mult, ALU.add, [B_xres[d], B_ctmp[tb]], [B_z[d]])
                ln_feed(d)
        ln_finish(f"lnmg{li}", f"lnmb{li}", "resid")

    def bview(i):
        return abf[:, i * KC * T:(i + 1) * KC * T].rearrange("p (c t) -> p c t", t=T)

    qt, qS, kt, khtok, vtok, gs, og = (bview(i) for i in range(7))
    khT = [abf[:, 7 * KC * T + i * T:7 * KC * T + (i + 1) * T] for i in range(2)]
    B_qt, B_qS, B_kt, B_khtok, B_vtok, B_gs, B_og = (bl(n, KC) for n in ("qt", "qS", "kt", "khtok", "vtok", "gs", "og"))
    B_khT = bl("khT", 2)
    zbq = [zb[:].rearrange("p c t -> p (c t)").bitcast(F32), zsq[:].rearrange("p c t -> p (c t)").bitcast(F32)]
    hf32 = [[af32[:, (par * 7 + i) * T:(par * 7 + i + 1) * T] for i in range(7)] for par in range(2)]
    hf32.append([zbq[i // 4][:, (i % 4) * T:(i % 4 + 1) * T] for i in range(7)])
    B_hf = [bl(f"hf{par}_", 7) for par in range(2)]
    B_hf.append([(B_zb[2 * i], B_zb[2 * i + 1]) if i < 4 else (B_zsq[2 * (i - 4)], B_zsq[2 * (i - 4) + 1])
                 for i in range(7)])
    rs4 = [af32[:, g * 7 * T:g * 7 * T + 4 * T].rearrange("p (h t) -> p h t", t=T) for g in range(2)]
    at2 = [abf[:, 7 * KC * T + 2 * T + i * T:7 * KC * T + 2 * T + (i + 1) * T] for i in range(2)]
    B_at = bl("at", 2)

    def hgrn_mixer(li, ja, lbi):
        S.soft_barrier()
        rot = Rot([0, 1, 2, 3, 4, 5, 6, 7])
        lb_ap = lbv[:, 2 * lbi, :]
        oml_ap = lbv[:, 2 * lbi + 1, :]
        St, Sb = Sst[ja], Sbf[ja]
        def bufs(h):
            par = h % 3
            return hf32[par], B_hf[par]

        def s0(h):
            (A_, K_, Bc, D1, E1, D2, QS), (bA, bK, bB, bD1, bE1, bD2, bQS) = bufs(h)
            w, wb = wget()
            w4 = w[:].rearrange("p (k s n) -> p k s n", s=4, n=P)
            bf_, bfb = rot.next()
            bq, bqb = rot.next()
            bg, bgb = rot.next()
            if h == 0:
                for k in range(KC):
                    mm(bf_[:], w4[:, k, 1, :], xbf[:, k, :], k == 0, k == KC - 1, [wb, B_xbf[k]], [bfb])
                    mm(bq[:], w4[:, k, 0, :], xbf[:, k, :], k == 0, k == KC - 1, [wb, B_xbf[k]], [bqb])
                    mm(bg[:], w4[:, k, 3, :], xbf[:, k, :], k == 0, k == KC - 1, [wb, B_xbf[k]], [bgb])
            else:
                for k in range(KC):
                    mm(bf_[:], w4[:, k, 1, :], xbf[:, k, :], k == 0, k == KC - 1, [wb, B_xbf[k]], [bfb])
            act(A_, bf_[:], AF.Sigmoid, [bfb], [bA])
            if h != 0:
                for k in range(KC):
                    mm(bq[:], w4[:, k, 0, :], xbf[:, k, :], k == 0, k == KC - 1, [wb, B_xbf[k]], [bqb])
            act(QS, bq[:], AF.Sigmoid, [bqb], [bQS])
            tt("vector", QS, bq[:], QS, ALU.mult, [bqb, bQS], [bQS])
            if h != 0:
                for k in range(KC):
                    mm(bg[:], w4[:, k, 3, :], xbf[:, k, :], k == 0, k == KC - 1, [wb, B_xbf[k]], [bgb])
            act(E1, bg[:], AF.Sigmoid, [bgb], [bE1])
            tt("vector", gs[:, h, :], bg[:], E1, ALU.mult, [bgb, bE1], [B_gs[h]])
            bv, bvb = rot.next()
            for tc in range(4):
                for k in range(KC):
                    mm(bv[:, tc * P:(tc + 1) * P], xbf[:, k, tc * P:(tc + 1) * P], w4[:, k, 2, :], k == 0, k == KC - 1,
                       [wb, B_xbf[k]], [bvb])
            cp("vector", vtok[:, h, :], bv[:], [bvb], [B_vtok[h]])

        def s1a(h):
            (A_, K_, Bc, D1, E1, D2, QS), (bA, bK, bB, bD1, bE1, bD2, bQS) = bufs(h)
            ts("vector", A_, A_, oml_ap[:, h:h + 1], lb_ap[:, h:h + 1], ALU.mult, ALU.add, [bA, B_lb], [bA])
            ts("gpsimd", K_, A_, -1.0, 1.0, ALU.mult, ALU.add, [bA], [bK])
            ts("vector", A_, A_, GATE_EPS, None, ALU.max, None, [bA], [bA])
            act(A_, A_, AF.Ln, [bA], [bA])

        def s1b(h):
            (A_, K_, Bc, D1, E1, D2, QS), (bA, bK, bB, bD1, bE1, bD2, bQS) = bufs(h)
            S.add("vector", (lambda Bc, A_: lambda e: e.tensor_tensor_scan(out=Bc, data0=carry[:], data1=A_, initial=0.0,
                                                                            op0=ALU.mult, op1=ALU.add))(Bc, A_),
                  [bA, B_const], [bB])
            b3 = Bc.rearrange("p (c j) -> p c j", j=CH)
            tt("gpsimd", D1.rearrange("p (c j) -> p c j", j=CH), b3, b3[:, :, CH // 2 - 1:CH // 2].to_broadcast([P, NCH, CH]),
               ALU.subtract, [bB], [bD1])
            tt("gpsimd", D2.rearrange("p (c j) -> p c j", j=CH), b3, b3[:, :, CH - 1:CH].to_broadcast([P, NCH, CH]),
               ALU.subtract, [bB], [bD2])

        def s2a(h):
            (A_, K_, Bc, D1, E1, D2, QS), (bA, bK, bB, bD1, bE1, bD2, bQS) = bufs(h)
            act(E1, D1, AF.Exp, [bD1], [bE1])
            act(D1, D1, AF.Exp, [bD1], [bD1], scale=-1.0)
            act(D2, D2, AF.Exp, [bD2], [bD2], scale=-1.0)
            act(Bc, Bc, AF.Exp, [bB], [bB])

        def s2b(h):
            (A_, K_, Bc, D1, E1, D2, QS), (bA, bK, bB, bD1, bE1, bD2, bQS) = bufs(h)
            b3 = Bc.rearrange("p (c j) -> p c j", j=CH)
            tt("vector", qt[:, h, :], QS, E1, ALU.mult, [bQS, bE1], [B_qt[h]])
            tt("vector", kt[:, h, :], K_, D1, ALU.mult, [bK, bD1], [B_kt[h]])
            kp = h % 2
            tt("gpsimd", khT[kp], K_, D2, ALU.mult, [bK, bD2], [B_khT[kp]])
            cp("gpsimd", dec[:, h, :], b3[:, :, CH - 1:CH].rearrange("p c o -> p (c o)"), [bB], [B_dec[h]])
            tt("gpsimd", qS[:, h, :], QS, Bc, ALU.mult, [bQS, bB], [B_qS[h]])
            bk, bkb = rot.next()
            for tc in range(4):
                mm(bk[:, tc * P:(tc + 1) * P], khT[kp][:, tc * P:(tc + 1) * P], ident_b[:], True, True,
                   [B_khT[kp], B_const], [bkb])
            cp("vector", khtok[:, h, :], bk[:], [bkb], [B_khtok[h]])

        for step in range(KC + 2):
            h1, h2 = step - 1, step - 2
            if 0 <= h2 < KC:
                s2a(h2)
            if 0 <= h1 < KC:
                s1a(h1)
                s1b(h1)
            if step < KC:
                s0(step)
            if 0 <= h2 < KC:
                s2b(h2)
        for c in range(NCH):
            blk, hf = c // 2, c % 2
            r0, r1 = hf * CH, (hf + 1) * CH
            par = c % 2
            X, Xb = banks[3 + par], bank_buf[3 + par]
            Y, Yb = banks[5 + par], bank_buf[5 + par]
            Zs = [(banks[0], bank_buf[0]), (banks[1], bank_buf[1])] if par == 0 else \
                 [(banks[2], bank_buf[2]), (banks[7], bank_buf[7])]
            a_sb, a_b = at2[par], B_at[par]
            for h in range(KC):
                mm(X[:, h * CH:(h + 1) * CH], kt[:, h, blk * P:(blk + 1) * P], qt[:, h, c * CH:(c + 1) * CH], True, True,
                   [B_kt[h], B_qt[h]], [Xb])
            tt("vector", a_sb[r0:r1, :].rearrange("p (h t) -> p h t", t=CH),
               X[r0:r1, :].rearrange("p (h t) -> p h t", t=CH),
               cmask[r0:r1, 0:1, :].to_broadcast([CH, KC, CH]), ALU.mult, [Xb, B_const], [a_b])
            for h in range(KC):
                Z, Zb = Zs[h // 4]
                hh = h % 4
                mm(Z[:, hh * P:(hh + 1) * P], khtok[r0:r1, h, blk * P:(blk + 1) * P],
                   vtok[r0:r1, h, blk * P:(blk + 1) * P], True, True, [B_khtok[h], B_vtok[h]], [Zb])
            for h in range(KC):
                mm(Y[:, h * CH:(h + 1) * CH], vtok[r0:r1, h, blk * P:(blk + 1) * P], a_sb[r0:r1, h * CH:(h + 1) * CH],
                   True, False, [B_vtok[h], a_b], [Yb])
                mm(Y[:, h * CH:(h + 1) * CH], Sb[:, h, :], qS[:, h, c * CH:(c + 1) * CH], False, True,
                   [B_Sbg[ja][h // 4], B_qS[h]], [Yb])
            act(zt[:, :, c * CH:(c + 1) * CH], Y[:].rearrange("p (h t) -> p h t", t=CH), AF.Copy, [Yb], B_z)
            for g in range(2):
                Z, Zb = Zs[g]
                for hh in range(4):
                    h = 4 * g + hh
                    stt(St[:, h, :], St[:, h, :], dec[:, h, c:c + 1], Z[:, hh * P:(hh + 1) * P], ALU.mult, ALU.add,
                        [B_S[ja][h], B_dec[h], Zb], [B_S[ja][h]])
                act(Sb[:, 4 * g:4 * g + 4, :], St[:, 4 * g:4 * g + 4, :], AF.Copy, B_S[ja][4 * g:4 * g + 4], [B_Sbg[ja][g]])
        ango = voff[f"ang{ja}"]
        for g in range(2):
            hs = slice(4 * g, 4 * g + 4)
            for hh in range(4):
                h = 4 * g + hh
                act(zsq[:, h, :], zt[:, h, :], AF.Square, [B_z[h]], [B_zsq[h]])
                mm(banks[h][:], ones_h[:], zsq[:, h, :], True, True, [B_zsq[h], B_const], [bank_buf[h]])
            rb = B_hf[g][0:4]
            act(rs4[g], ps_all[:, hs, :], AF.Ln, bank_buf[hs] + [B_const], rb, bias=epsc[:, 1:2])
            act(rs4[g], rs4[g], AF.Exp, rb, rb, scale=-0.5)
            for hh in range(4):
                h = 4 * g + hh
                stt(zt[:, h, :], zt[:, h, :], vecs[:, ango + h:ango + h + 1], gs[:, h, :], ALU.mult, ALU.mult,
                    [B_z[h], B_gs[h], B_vecs], [B_z[h]])
            tt("vector", og[:, hs, :], zt[:, hs, :], rs4[g], ALU.mult, B_z[hs] + rb, B_og[hs])

        rot = Rot([0, 1, 2, 3, 4, 5])
        for s in range(2):
            w, wb = wget()
            w3 = w[:].rearrange("p (k n) -> p k n", n=512)
            grp = [(j,) + rot.next() for j in range(4)]
            if s == 0:
                for k in range(KC):
                    for j, bank, bbuf in grp:
                        mm(bank[:], w3[:, k, j * P:(j + 1) * P], og[:, k, :], k == 0, k == KC - 1, [wb, B_og[k]], [bbuf])
            for j, bank, bbuf in grp:
                d = 4 * s + j
                if s != 0:
                    for k in range(KC):
                        mm(bank[:], w3[:, k, j * P:(j + 1) * P], og[:, k, :], k == 0, k == KC - 1, [wb, B_og[k]], [bbuf])
                flush_stats()
                resid_from_psum(d, bank, bbuf)
        ln_finish(f"lnmg{li}", f"lnmb{li}", "resid")

    for lst in (B_hT, B_ftmp, B_u, B_cb, B_ctmp, B_qt, B_qS, B_kt, B_khtok, B_vtok, B_gs, B_og, B_khT,
                B_hf[0], B_hf[1], B_at):
        for b_ in lst:
            b_.arena = True

    xres_flat = xres[:].rearrange("p c t -> p (c t)")
    for it in range(n_tiles):
        for d in range(KC):
            src = zbq[d // 4][:, (d % 4) * T:(d % 4 + 1) * T]
            sbuf_ = (B_zb if d < 4 else B_zsq)[2 * (d % 4):2 * (d % 4) + 2]
            cp("vector" if d % 2 == 0 else "scalar", xbf[:, d, :], src, sbuf_, [B_xbf[d]])
        for d in range(KC):
            src = zbq[d // 4][:, (d % 4) * T:(d % 4 + 1) * T]
            sbuf_ = (B_zb if d < 4 else B_zsq)[2 * (d % 4):2 * (d % 4) + 2]
            cp("gpsimd" if d % 2 == 0 else "scalar", xres[:, d, :], src, sbuf_, [B_xres[d]])
        ja = jb = 0
        for li, k in enumerate(kinds):
            if it == 0:
                for pi, l2 in piece_layer:
                    if l2 == li + 1:
                        issue_cast(pi)
            if k == "A":
                hgrn_mixer(li, ja, ja)
                ja += 1
            else:
                conv_mixer(li, jb)
                jb += 1
            if li == L - 1 and it + 1 < n_tiles:
                ffn(li, hook=(lambda it: lambda: issue_xload(it + 1))(it))
            else:
                ffn(li)
        S.add("gpsimd", (lambda it: lambda e: e.dma_start(out=out_d[it], in_=xres_flat))(it), B_xres, (), dma_key="xout")
    S.add("gpsimd", lambda e: None, (), (), extra=[S.dma_last["xout"]])
    S.finalize()

    eng_sems = {e: es.enter_context(nc.semaphore(f"sem_{e}")) for e in ENGS}
    dma_sems = {k: es.enter_context(nc.semaphore(f"dsem_{k}")) for k in S.dma_count}
    with nc.Block() as block:
        @block.sync
        def _(e):
            S.emit("sync", e, eng_sems, dma_sems)

        @block.tensor
        def _(e):
            S.emit("tensor", e, eng_sems, dma_sems)

        @block.vector
        def _(e):
            S.emit("vector", e, eng_sems, dma_sems)

        @block.scalar
        def _(e):
            S.emit("scalar", e, eng_sems, dma_sems)

        @block.gpsimd
        def _(e):
            S.emit("gpsimd", e, eng_sems, dma_sems)
    es.close()
    return nc, S


def x_to_fm(xb, n_tiles):
    a = xb.reshape(n_tiles, T, KC, P).transpose(0, 3, 2, 1)
    return np.ascontiguousarray(a).reshape(n_tiles, P, KC * T)


def fm_to_x(o, n_tiles):
    a = o.reshape(n_tiles, P, KC, T).transpose(0, 3, 2, 1)
    return np.ascontiguousarray(a).reshape(n_tiles * T, D)


def run(x, inp, kinds, layer_ids, core_ids, placement=None):
    x = np.asarray(x, np.float32)
    nb, seqlen, _ = x.shape
    n_tiles = seqlen // T
    wslabs, vecs, seq, nh = pack_weights(kinds, layer_ids, inp)
    nc, _ = build(n_tiles, kinds)
    if placement is None:
        placement = list(range(nb))
    ncore = len(core_ids)
    in_maps = [None] * ncore
    for b, c in enumerate(placement):
        in_maps[c] = {"x_fm": x_to_fm(x[b], n_tiles), "wslabs": wslabs, "vecs": vecs}
    idle = None
    for c in range(ncore):
        if in_maps[c] is None:
            if idle is None:
                idle = {"x_fm": np.zeros((n_tiles, P, KC * T), np.float32), "wslabs": np.zeros_like(wslabs),
                        "vecs": np.zeros_like(vecs)}
            in_maps[c] = idle
    res = run_bass_kernel_spmd(nc, in_maps, core_ids=core_ids)
    return np.stack([fm_to_x(np.asarray(res.results[c]["out_fm"]), n_tiles) for c in placement])


def kernel(**inputs):
    x = np.asarray(inputs["x"], np.float32)
    kinds = ["A", "B", "A", "B"]
    out = run(x, inputs, kinds, [0, 1, 2, 3], list(range(x.shape[0])))
    return out.astype(np.float32)
```

```python
import numpy as np
from contextlib import ExitStack
import concourse.bass as bass
import concourse.mybir as mybir
from concourse.bass_utils import run_bass_kernel_spmd

F32 = mybir.dt.float32
BF16 = mybir.dt.bfloat16
AF = mybir.ActivationFunctionType
ALU = mybir.AluOpType

P = 128
D = 1024
KC = 8
DFF = 4096
T = 512
CH = 64
NCH = T // CH
KW = 31
SLAB = 4096
ALPHA = 8.0 ** 0.25
LN_EPS = 1e-5
RMS_EPS = 1e-6
GATE_EPS = 1e-6
NSLOT = 4
N_CORES = 4


class Buf:
    __slots__ = ("name", "w", "r", "arena")

    def __init__(self, name):
        self.name = name
        self.w = None
        self.r = []
        self.arena = False


def _flat(xs):
    out = []
    for b in xs:
        if isinstance(b, (tuple, list)):
            out.extend(_flat(b))
        else:
            out.append(b)
    return out


class Op:
    __slots__ = ("eng", "fn", "deps", "idx", "sig", "sigval", "is_dma", "key", "dma_val")


COMPUTE = ("tensor", "vector", "scalar", "gpsimd")
ENGS = ("sync",) + COMPUTE


class Sched:
    def __init__(self):
        self.ops = {e: [] for e in ENGS}
        self.last = {e: None for e in ENGS}
        self.pending = {e: set() for e in ENGS}
        self.dma_count = {}
        self.dma_last = {}
        self.arena_last = {e: None for e in ENGS}
        self.arena_pending = {e: set() for e in ENGS}

    def soft_barrier(self):
        snap = [self.arena_last[e] for e in COMPUTE if self.arena_last[e] is not None]
        for e in COMPUTE:
            self.arena_pending[e].update(snap)

    def add(self, eng, fn, reads=(), writes=(), dma_key=None, extra=()):
        reads = _flat(reads)
        writes = _flat(writes)
        op = Op()
        op.eng = eng
        op.fn = fn
        op.sig = False
        op.sigval = 0
        op.is_dma = dma_key is not None
        op.key = dma_key
        deps = set(extra)
        for b in reads:
            if b.w is not None:
                deps.add(b.w)
        for b in writes:
            if b.w is not None:
                deps.add(b.w)
            deps.update(b.r)
        for b in writes:
            b.w = op
            b.r = []
        for b in reads:
            b.r.append(op)
        if self.pending[eng]:
            deps.update(self.pending[eng])
            self.pending[eng] = set()
        if any(b.arena for b in reads) or any(b.arena for b in writes):
            if self.arena_pending[eng]:
                deps.update(self.arena_pending[eng])
                self.arena_pending[eng] = set()
            self.arena_last[eng] = op
        if op.is_dma:
            prev = self.dma_last.get(dma_key)
            if prev is not None:
                deps.add(prev)
            self.dma_count[dma_key] = self.dma_count.get(dma_key, 0) + 16
            op.dma_val = self.dma_count[dma_key]
            self.dma_last[dma_key] = op
        deps.discard(op)
        op.deps = deps
        op.idx = len(self.ops[eng])
        self.ops[eng].append(op)
        self.last[eng] = op
        return op

    def barrier(self):
        snap = [self.last[e] for e in COMPUTE if self.last[e] is not None]
        for e in COMPUTE:
            self.pending[e].update(o for o in snap if o.eng != e)

    def finalize(self):
        for e in ENGS:
            for op in self.ops[e]:
                for d in op.deps:
                    if not d.is_dma and (d.eng != e or e != "tensor"):
                        d.sig = True
        for e in ENGS:
            n = 0
            for op in self.ops[e]:
                if op.sig and not op.is_dma:
                    n += 1
                    op.sigval = n

    def emit(self, eng_name, eng, eng_sems, dma_sems):
        waited = {}
        for op in self.ops[eng_name]:
            need = {}
            for d in op.deps:
                if d.is_dma:
                    k = ("dma", d.key)
                    v = d.dma_val
                elif d.eng == eng_name and eng_name == "tensor":
                    continue
                else:
                    k = d.eng
                    v = d.sigval
                if need.get(k, 0) < v:
                    need[k] = v
            for k, v in need.items():
                if waited.get(k, 0) >= v:
                    continue
                waited[k] = v
                sem = dma_sems[k[1]] if isinstance(k, tuple) else eng_sems[k]
                eng.wait_ge(sem, v)
            ins = op.fn(eng)
            if ins is None:
                continue
            if op.is_dma:
                ins.then_inc(dma_sems[op.key], 16)
            elif op.sig:
                ins.then_inc(eng_sems[eng_name], 1)


def vec_layout(kinds):
    off = {}
    n = 0

    def put(name, cols):
        nonlocal n
        off[name] = n
        n += cols

    for i in range(len(kinds)):
        for nm in ("lnmg", "lnmb", "lnfg", "lnfb"):
            put(f"{nm}{i}", KC)
    put("lbl", 2 * KC)
    ja = jb = 0
    for k in kinds:
        if k == "A":
            put(f"ang{ja}", KC)
            ja += 1
        else:
            put(f"bpw1{jb}", 2 * KC)
            for nm in ("bdw", "blg", "blb", "bpw2"):
                put(f"{nm}{jb}", KC)
            put(f"wdw{jb}", KC * KW)
            jb += 1
    return off, n


def fm_vec(v):
    v = np.asarray(v, np.float32)
    return np.ascontiguousarray(v.reshape(-1, P).T)


def colslabs(w, ns):
    K, N = w.shape
    kc = K // P
    a = w.reshape(kc, P, N // ns, ns).transpose(2, 1, 0, 3)
    return np.ascontiguousarray(a).reshape(N // ns, P, kc * ns)


def pack_weights(kinds, layer_ids, inp):
    slabs = []
    seq = []
    conv_ids = {}
    off, nv = vec_layout(kinds)
    vecs = np.zeros((P, nv), np.float32)
    lbl = np.asarray(inp["a_lb_logits"], np.float32)
    vecs[:, off["lbl"]:off["lbl"] + 2 * KC] = fm_vec(lbl.reshape(-1))
    n_conv = 0
    pending_conv = []
    ja = jb = 0
    for li, k in enumerate(kinds):
        i = layer_ids[li]
        j = i // 2
        vecs[:, off[f"lnmg{li}"]:off[f"lnmg{li}"] + KC] = fm_vec(inp["ln_mix_g"][i])
        vecs[:, off[f"lnmb{li}"]:off[f"lnmb{li}"] + KC] = fm_vec(inp["ln_mix_b"][i])
        vecs[:, off[f"lnfg{li}"]:off[f"lnfg{li}"] + KC] = fm_vec(inp["ln_ffn_g"][i])
        vecs[:, off[f"lnfb{li}"]:off[f"lnfb{li}"] + KC] = fm_vec(inp["ln_ffn_b"][i])
        if k == "A":
            w_in = np.asarray(inp["a_w_in"][j], np.float32)
            a = w_in.reshape(KC, P, 4, KC, P).transpose(3, 1, 0, 2, 4)
            a = np.ascontiguousarray(a).reshape(KC, P, SLAB)
            for h in range(KC):
                seq.append(len(slabs))
                slabs.append(a[h])
            for s in colslabs(np.asarray(inp["a_w_out"][j], np.float32), 512):
                seq.append(len(slabs))
                slabs.append(s)
            vecs[:, off[f"ang{ja}"]:off[f"ang{ja}"] + KC] = fm_vec(inp["a_norm_g"][j])
            ja += 1
        else:
            w1 = np.asarray(inp["b_w_pw1"][j], np.float32)
            a = w1.reshape(KC, P, 2, 4, 2, P).transpose(3, 1, 0, 2, 4, 5)
            a = np.ascontiguousarray(a).reshape(4, P, SLAB)
            for s in range(4):
                seq.append(len(slabs))
                slabs.append(a[s])
            for c in range(KC):
                seq.append(("conv", jb, c))
            for s in colslabs(np.asarray(inp["b_w_pw2"][j], np.float32), 512):
                seq.append(len(slabs))
                slabs.append(s)
            vecs[:, off[f"bpw1{jb}"]:off[f"bpw1{jb}"] + 2 * KC] = fm_vec(inp["b_b_pw1"][j])
            vecs[:, off[f"bdw{jb}"]:off[f"bdw{jb}"] + KC] = fm_vec(inp["b_b_dw"][j])
            vecs[:, off[f"blg{jb}"]:off[f"blg{jb}"] + KC] = fm_vec(inp["b_ln_g"][j])
            vecs[:, off[f"blb{jb}"]:off[f"blb{jb}"] + KC] = fm_vec(inp["b_ln_b"][j])
            vecs[:, off[f"bpw2{jb}"]:off[f"bpw2{jb}"] + KC] = fm_vec(inp["b_b_pw2"][j])
            wdw = np.asarray(inp["b_w_dw"][j], np.float32)
            vecs[:, off[f"wdw{jb}"]:off[f"wdw{jb}"] + KC * KW] = np.ascontiguousarray(
                wdw.reshape(KW, KC, P).transpose(2, 1, 0)).reshape(P, KC * KW)
            jb += 1
        for s in colslabs(np.asarray(inp["ffn_w1"][i], np.float32), 512):
            seq.append(len(slabs))
            slabs.append(s)
        for s in colslabs(np.asarray(inp["ffn_w2"][i], np.float32), 128):
            seq.append(len(slabs))
            slabs.append(s)
    nh = len(slabs)
    seq2 = []
    for s in seq:
        if isinstance(s, tuple):
            seq2.append(nh + s[1] * KC + s[2])
        else:
            seq2.append(s)
    return np.stack(slabs), vecs, seq2, nh


def slab_seq(kinds):
    seq = []
    n = 0
    jb = 0
    for k in kinds:
        if k == "A":
            cnt = [8, 2]
        else:
            cnt = [4, ("conv", jb), 2]
            jb += 1
        for c in cnt + [8, 8]:
            if isinstance(c, tuple):
                seq.extend(("conv", c[1], x) for x in range(KC))
            else:
                seq.extend(range(n, n + c))
                n += c
    nh = n
    return [nh + s[1] * KC + s[2] if isinstance(s, tuple) else s for s in seq], nh


def build(n_tiles, kinds):
    nc = bass.Bass("TRN2", target_bir_lowering=False)
    L = len(kinds)
    nA = sum(1 for k in kinds if k == "A")
    nB = L - nA
    voff, NV = vec_layout(kinds)
    seq, NH = slab_seq(kinds)
    NS = NH + nB * KC

    x_in = nc.dram_tensor("x_fm", [n_tiles, P, KC * T], F32, kind="ExternalInput").ap()
    w_in_d = nc.dram_tensor("wslabs", [NH, P, SLAB], F32, kind="ExternalInput").ap()
    vecs_d = nc.dram_tensor("vecs", [P, NV], F32, kind="ExternalInput").ap()
    out_d = nc.dram_tensor("out_fm", [n_tiles, P, KC * T], F32, kind="ExternalOutput").ap()
    wsc = nc.dram_tensor("wsc", [NS, P, SLAB], BF16).ap()

    S = Sched()
    es = ExitStack()

    def sb(name, shape, dt):
        return es.enter_context(nc.sbuf_tensor(name, shape, dt))

    xres = sb("xres", [P, KC, T], F32)
    xbf = sb("xbf", [P, KC, T], BF16)
    zt = sb("zt", [P, KC, T], F32)
    zb = sb("zb", [P, KC, T], BF16)
    zsq = sb("zsq", [P, KC, T], BF16)
    st_mean = sb("st_mean", [P, T], F32)
    st_tmp = sb("st_tmp", [P, T], F32)
    st_rstd = sb("st_rstd", [P, T], F32)
    wslot = [sb(f"wslot{i}", [P, SLAB], BF16) for i in range(NSLOT)]
    vecs = sb("vecs_sb", [P, NV], F32)
    lbv = sb("lbv", [P, 4, KC], F32)
    ident_f = sb("ident_f", [P, P], F32)
    ident_b = sb("ident_b", [P, P], BF16)
    ones_d = sb("ones_d", [P, P], BF16)
    ones_h = sb("ones_h", [P, P], BF16)
    cmask = sb("cmask", [P, 1, CH], F32)
    epsc = sb("epsc", [P, 2], F32)
    carry = sb("carry", [P, T], F32)
    Sst = [sb(f"Sst{j}", [P, KC, P], F32) for j in range(max(nA, 1))]
    Sbf = [sb(f"Sbf{j}", [P, KC, P], BF16) for j in range(max(nA, 1))]
    halo = [sb(f"halo{j}", [P, KC, KW - 1], BF16) for j in range(max(nB, 1))]
    dec = sb("dec", [P, KC, NCH], F32)
    NBF = 30720
    NF32 = 7168
    abf = sb("abf", [P, NBF], BF16)
    af32 = sb("af32", [P, NF32], F32)

    ps_all = es.enter_context(nc.psum_tensor("ps_all", [P, 8, T], F32))
    banks = [ps_all[:, i, :] for i in range(8)]
    bank_buf = [Buf(f"bank{i}") for i in range(8)]

    class Rot:
        def __init__(self, ids):
            self.ids = ids
            self.i = 0

        def next(self):
            b = self.ids[self.i % len(self.ids)]
            self.i += 1
            return banks[b], bank_buf[b]

    def bl(name, n):
        return [Buf(f"{name}{i}") for i in range(n)]

    B_xres = bl("xres", KC)
    B_xbf = bl("xbf", KC)
    B_z = bl("z", KC)
    B_zb = bl("zb", KC)
    B_zsq = bl("zsq", KC)
    B_mean = Buf("mean")
    B_tmp = Buf("sttmp")
    B_rstd = Buf("rstd")
    B_slot = bl("slot", NSLOT)
    B_vecs = Buf("vecs")
    B_const = Buf("const")
    B_cast = bl("cast", L)
    B_conv = {}
    B_S = [bl(f"S{j}_", KC) for j in range(max(nA, 1))]
    B_Sbg = [bl(f"Sbf{j}_", 2) for j in range(max(nA, 1))]
    B_halo = bl("halo", max(nB, 1))
    B_dec = bl("dec", KC)

    def mm(out, lhsT, rhs, start, stop, reads, writes):
        return S.add("tensor", lambda e: e.matmul(out, lhsT=lhsT, rhs=rhs, start=start, stop=stop), reads, writes)

    def act(out, in_, func, reads, writes, bias=None, scale=None):
        kw = {}
        if bias is not None:
            kw["bias"] = bias
        if scale is not None:
            kw["scale"] = scale
        return S.add("scalar", lambda e: e.activation(out=out, in_=in_, func=func, **kw), reads, writes)

    def tt(eng, out, in0, in1, op, reads, writes):
        return S.add(eng, lambda e: e.tensor_tensor(out=out, in0=in0, in1=in1, op=op), reads, writes)

    def ts(eng, out, in0, s1, s2, op0, op1, reads, writes):
        if op1 is None:
            return S.add(eng, lambda e: e.tensor_scalar(out, in0, s1, None, op0), reads, writes)
        return S.add(eng, lambda e: e.tensor_scalar(out, in0, s1, s2, op0, op1), reads, writes)

    def stt(out, in0, scalar, in1, op0, op1, reads, writes):
        return S.add("vector", lambda e: e.scalar_tensor_tensor(out=out, in0=in0, scalar=scalar, in1=in1,
                                                                 op0=op0, op1=op1), reads, writes)

    def cp(eng, out, in_, reads, writes):
        if eng == "scalar":
            return act(out, in_, AF.Copy, reads, writes)
        return S.add(eng, lambda e: e.tensor_copy(out=out, in_=in_), reads, writes)

    def rsqrt(out, in_, epsi, reads, writes):
        act(out, in_, AF.Ln, list(reads) + [B_const], writes, bias=epsc[:, epsi:epsi + 1])
        act(out, out, AF.Exp, writes, writes, scale=-0.5)

    def memset(eng, ap, val, writes):
        return S.add(eng, lambda e: e.memset(ap, val), (), writes)

    def vcol(name, c=0):
        o = voff[name] + c
        return vecs[:, o:o + 1]

    wstate = {"issued": 0, "taken": 0}
    total_loads = n_tiles * len(seq)

    def slab_src_buf(sidx):
        if sidx >= NH:
            return B_conv[sidx]
        return B_piece[slab_piece[sidx]]

    slab_layer = {}
    _n = 0
    for li, k in enumerate(kinds):
        cnt = (8 + 2 if k == "A" else 4 + 2) + 16
        for s in range(_n, _n + cnt):
            slab_layer[s] = li
        _n += cnt
    assert _n == NH

    def issue_load(j):
        sidx = seq[j % len(seq)]
        slot = j % NSLOT
        if sidx >= NH:
            dst = wslot[slot][:, 0:KW * P]
            src = wsc[sidx][:, 0:KW * P]
        else:
            dst = wslot[slot][:]
            src = wsc[sidx]
        S.add("sync", lambda e: e.dma_start(out=dst, in_=src), [slab_src_buf(sidx)], [B_slot[slot]],
              dma_key=f"slot{slot}")

    def wget(la=NSLOT):
        i = wstate["taken"]
        while wstate["issued"] < min(i + la, total_loads):
            issue_load(wstate["issued"])
            wstate["issued"] += 1
        wstate["taken"] += 1
        return wslot[i % NSLOT], B_slot[i % NSLOT]

    S.add("gpsimd", lambda e: e.dma_start(out=vecs[:], in_=vecs_d), (), [B_vecs], dma_key="vecs")
    HT = KC * T // 2
    zbq = [zb[:].rearrange("p c t -> p (c t)").bitcast(F32), zsq[:].rearrange("p c t -> p (c t)").bitcast(F32)]

    def issue_xload(it):
        S.add("gpsimd", (lambda it: lambda e: e.dma_start(out=zbq[0], in_=x_in[it][:, 0:HT]))(it), (), B_zb, dma_key="xin0")
        S.add("gpsimd", (lambda it: lambda e: e.dma_start(out=zbq[1], in_=x_in[it][:, HT:2 * HT]))(it), (), B_zsq,
              dma_key="xin1")

    issue_xload(0)
    pieces = []
    lo = 0
    for li, k in enumerate(kinds):
        cnt = (8 + 2 if k == "A" else 4 + 2) + 16
        if li == 0:
            cuts = [lo, lo + 1, lo + 4, lo + 10, lo + cnt]
        else:
            cuts = [lo, lo + cnt]
        for a, b in zip(cuts[:-1], cuts[1:]):
            pieces.append((a, b))
        lo += cnt
    B_piece = bl("castp", len(pieces))
    slab_piece = {}
    piece_layer = []
    lo_ = 0
    for li, k in enumerate(kinds):
        cnt = (8 + 2 if k == "A" else 4 + 2) + 16
        for pi, (a, b) in enumerate(pieces):
            if lo_ <= a < lo_ + cnt:
                piece_layer.append((pi, li))
        lo_ += cnt

    def issue_cast(pi):
        a, b = pieces[pi]
        for sidx in range(a, b):
            slab_piece[sidx] = pi
        S.add("gpsimd", (lambda a, b: lambda e: e.dma_start(
            out=wsc[a:b].rearrange("s p n -> (s p) n"), in_=w_in_d[a:b].rearrange("s p n -> (s p) n")))(a, b),
            (), [B_piece[pi]], dma_key=f"cast{pi}")

    for pi, li in piece_layer:
        if li == 0:
            issue_cast(pi)
    memset("vector", ident_f[:], 1.0, [B_const])
    S.add("gpsimd", lambda e: e.affine_select(out=ident_f[:], in_=ident_f[:], pattern=[[-1, P]],
                                              compare_op=ALU.is_equal, fill=0.0, base=0, channel_multiplier=1),
          [B_const], [B_const])
    cp("vector", ident_b[:], ident_f[:], [B_const], [B_const])
    memset("vector", ones_d[:], 1.0 / D, [B_const])
    memset("vector", ones_h[:], 1.0 / P, [B_const])
    memset("vector", cmask[:], 1.0, [B_const])
    for hf in range(2):
        S.add("gpsimd", (lambda hf: lambda e: e.affine_select(
            out=cmask[hf * CH:(hf + 1) * CH, 0, :], in_=cmask[hf * CH:(hf + 1) * CH, 0, :], pattern=[[1, CH]],
            compare_op=ALU.is_ge, fill=0.0, base=0, channel_multiplier=-1))(hf), [B_const], [B_const])
    memset("vector", epsc[:, 0:1], LN_EPS, [B_const])
    memset("vector", epsc[:, 1:2], RMS_EPS, [B_const])
    memset("vector", carry[:], 1.0, [B_const])
    memset("vector", carry[:].rearrange("p (c j) -> p c j", j=CH)[:, :, 0:1], 0.0, [B_const])
    for j in range(max(nA, 1)):
        memset("vector", Sst[j][:], 0.0, B_S[j])
        memset("gpsimd", Sbf[j][:], 0.0, B_Sbg[j])
    for j in range(max(nB, 1)):
        memset("gpsimd", halo[j][:], 0.0, [B_halo[j]])
    B_lb = Buf("lb")
    l0 = vecs[:, voff["lbl"]:voff["lbl"] + KC]
    l1 = vecs[:, voff["lbl"] + KC:voff["lbl"] + 2 * KC]
    tt("vector", lbv[:, 1, :], l1, l0, ALU.subtract, [B_vecs], [B_lb])
    act(lbv[:, 2, :], lbv[:, 1, :], AF.Sigmoid, [B_lb], [B_lb])
    act(lbv[:, 3, :], lbv[:, 1, :], AF.Sigmoid, [B_lb], [B_lb], scale=-1.0)
    tt("vector", lbv[:, 0, :], lbv[:, 3, :], lbv[:, 3, :], ALU.subtract, [B_lb], [B_lb])
    ts("vector", lbv[:, 1, :], lbv[:, 0, :], -1.0, 1.0, ALU.mult, ALU.add, [B_lb], [B_lb])
    ts("vector", lbv[:, 3, :], lbv[:, 2, :], -1.0, 1.0, ALU.mult, ALU.add, [B_lb], [B_lb])
    dgs = sb("dgs", [P, 16 * P], BF16)
    B_dgs = Buf("dgs")
    diag_jobs = []
    for jb in range(nB):
        for c in range(KC):
            sidx = NH + jb * KC + c
            B_conv[sidx] = Buf(f"conv{sidx}")
            for half in range(2):
                def job(jb=jb, c=c, sidx=sidx, half=half):
                    k0 = 16 * half
                    nk = 16 if half == 0 else KW - 16
                    o = voff[f"wdw{jb}"] + c * KW + k0
                    tt("vector", dgs[:, 0:nk * P].rearrange("p (k m) -> p k m", m=P),
                       ident_f[:].unsqueeze(1).to_broadcast([P, nk, P]),
                       vecs[:, o:o + nk].unsqueeze(2).to_broadcast([P, nk, P]), ALU.mult,
                       [B_vecs, B_const], [B_dgs])
                    S.add("scalar", lambda e: e.dma_start(out=wsc[sidx][:, k0 * P:(k0 + nk) * P], in_=dgs[:, 0:nk * P]),
                          [B_dgs], [B_conv[sidx]], dma_key="dgs")
                diag_jobs.append(job)
    S.barrier()

    stat1 = (banks[6], bank_buf[6])
    stat2 = (banks[7], bank_buf[7])

    pend = []

    def ln_feed(d):
        act(zb[:, d, :], zt[:, d, :], AF.Copy, [B_z[d]], [B_zb[d]])
        act(zsq[:, d, :], zt[:, d, :], AF.Square, [B_z[d]], [B_zsq[d]])
        pend.append(d)

    def flush_stats():
        for d in pend:
            mm(stat1[0][:], ones_d[:], zb[:, d, :], d == 0, d == KC - 1, [B_zb[d], B_const], [stat1[1]])
            mm(stat2[0][:], ones_d[:], zsq[:, d, :], d == 0, d == KC - 1, [B_zsq[d], B_const], [stat2[1]])
        pend.clear()

    def ln_finish(gname, bname, mode, dst=None, B_dst=None, hook=None):
        flush_stats()
        if hook is not None:
            hook()
        act(st_tmp[:], stat1[0][:], AF.Square, [stat1[1]], [B_tmp])
        tt("vector", st_tmp[:], stat2[0][:], st_tmp[:], ALU.subtract, [stat2[1], B_tmp], [B_tmp])
        rsqrt(st_rstd[:], st_tmp[:], 0, [B_tmp], [B_rstd])
        for dp in range(KC // 2):
            ds_ = slice(2 * dp, 2 * dp + 2)
            tt("vector", zt[:, ds_, :], zt[:, ds_, :], stat1[0].unsqueeze(1).to_broadcast([P, 2, T]), ALU.subtract,
               B_z[ds_] + [stat1[1]], B_z[ds_])
            tt("vector", zt[:, ds_, :], zt[:, ds_, :], st_rstd[:].unsqueeze(1).to_broadcast([P, 2, T]), ALU.mult,
               B_z[ds_] + [B_rstd], B_z[ds_])
        order = list(range(KC))
        for d in order:
            if mode == "resid":
                act(xbf[:, d, :], zt[:, d, :], AF.Identity, [B_z[d], B_vecs], [B_xbf[d]],
                    bias=vcol(bname, d), scale=vcol(gname, d))
            else:
                act(dst[:, d, :], zt[:, d, :], AF.Silu, [B_z[d], B_vecs], [B_dst[d]],
                    bias=vcol(bname, d), scale=vcol(gname, d))
        if mode == "resid":
            for d in order:
                if d % 2 == 0:
                    act(xres[:, d, :], zt[:, d, :], AF.Identity, [B_z[d], B_vecs], [B_xres[d]],
                        bias=vcol(bname, d), scale=vcol(gname, d))
                else:
                    ts("vector", xres[:, d, :], zt[:, d, :], vcol(gname, d), vcol(bname, d), ALU.mult, ALU.add,
                       [B_z[d], B_vecs], [B_xres[d]])

    def resid_from_psum(d, bank, bbuf):
        stt(zt[:, d, :], xres[:, d, :], ALPHA, bank[:], ALU.mult, ALU.add, [B_xres[d], bbuf], [B_z[d]])
        ln_feed(d)

    hT = abf[:, 0:32 * T].rearrange("p (c t) -> p c t", t=T)
    B_hT = bl("hT", 32)
    ftmp = [af32[:, i * T:(i + 1) * T] for i in range(3)]
    B_ftmp = bl("ftmp", 3)

    def ffn(li, hook=None):
        S.soft_barrier()
        rot = Rot([0, 1, 2, 3, 4, 5])
        n = 0
        NW = 6
        slabs = [wget(), wget(la=NSLOT - 1)]

        def evac(f, bank, bbuf):
            nonlocal n
            tb = n % 3
            act(ftmp[tb], bank[:], AF.Relu, [bbuf], [B_ftmp[tb]])
            tt("vector", hT[:, f, :], ftmp[tb], ftmp[tb], ALU.mult,
               [B_ftmp[tb]], [B_hT[f]])
            n += 1
            if diag_jobs:
                diag_jobs.pop(0)()

        wave = []
        for f in range(NW):
            w, wb = slabs[f // 4]
            w3 = w[:].rearrange("p (k n) -> p k n", n=512)
            bank, bbuf = rot.next()
            wave.append((f, w3, wb, bank, bbuf))
        for k in range(KC):
            for f, w3, wb, bank, bbuf in wave:
                j = f % 4
                mm(bank[:], w3[:, k, j * P:(j + 1) * P], xbf[:, k, :], k == 0, k == KC - 1, [wb, B_xbf[k]], [bbuf])
        for f, w3, wb, bank, bbuf in wave:
            evac(f, bank, bbuf)
        for s in range(1, 8):
            w, wb = slabs[s] if s < 2 else wget()
            w3 = w[:].rearrange("p (k n) -> p k n", n=512)
            for j in range(4):
                f = 4 * s + j
                if f < NW:
                    continue
                bank, bbuf = rot.next()
                for k in range(KC):
                    mm(bank[:], w3[:, k, j * P:(j + 1) * P], xbf[:, k, :], k == 0, k == KC - 1, [wb, B_xbf[k]], [bbuf])
                evac(f, bank, bbuf)
        for d in range(KC):
            w, wb = wget()
            w3 = w[:].rearrange("p (k n) -> p k n", n=P)
            bank, bbuf = rot.next()
            for f in range(32):
                mm(bank[:], w3[:, f, :], hT[:, f, :], f == 0, f == 31, [wb, B_hT[f]], [bbuf])
            flush_stats()
            resid_from_psum(d, bank, bbuf)
        ln_finish(f"lnfg{li}", f"lnfb{li}", "resid", hook=hook)

    UW = T + KW - 1
    ubuf = abf[:, 0:KC * UW].rearrange("p (c t) -> p c t", t=UW)
    cbuf = abf[:, KC * UW:KC * UW + KC * T].rearrange("p (c t) -> p c t", t=T)
    B_u = bl("u", KC)
    B_cb = bl("cb", KC)
    ctmp = [af32[:, i * T:(i + 1) * T] for i in range(3)]
    B_ctmp = bl("ctmp", 3)
    cpart = [af32[:, (3 + i) * T:(4 + i) * T] for i in range(KC)]
    B_cpart = bl("cpart", KC)

    def conv_mixer(li, jb):
        while diag_jobs:
            diag_jobs.pop(0)()
        S.soft_barrier()
        rot = Rot([0, 1, 2, 3, 4, 5])
        cp("gpsimd", ubuf[:, :, 0:KW - 1], halo[jb][:], [B_halo[jb]], B_u)
        n = 0
        for s in range(4):
            w, wb = wget()
            w4 = w[:].rearrange("p (k q n) -> p k q n", q=4, n=P)
            grp = []
            for j in range(2):
                ba, bab = rot.next()
                bg, bgb = rot.next()
                grp.append((j, ba, bab, bg, bgb))
            if s == 0:
                for k in range(KC):
                    for j, ba, bab, bg, bgb in grp:
                        mm(ba[:], w4[:, k, j, :], xbf[:, k, :], k == 0, k == KC - 1, [wb, B_xbf[k]], [bab])
                        mm(bg[:], w4[:, k, 2 + j, :], xbf[:, k, :], k == 0, k == KC - 1, [wb, B_xbf[k]], [bgb])
            for j, ba, bab, bg, bgb in grp:
                c = 2 * s + j
                if s != 0:
                    for k in range(KC):
                        mm(ba[:], w4[:, k, j, :], xbf[:, k, :], k == 0, k == KC - 1, [wb, B_xbf[k]], [bab])
                    for k in range(KC):
                        mm(bg[:], w4[:, k, 2 + j, :], xbf[:, k, :], k == 0, k == KC - 1, [wb, B_xbf[k]], [bgb])
                tb = n % 3
                n += 1
                act(ctmp[tb], bg[:], AF.Sigmoid, [bgb, B_vecs], [B_ctmp[tb]], bias=vcol(f"bpw1{jb}", KC + c))
                stt(ubuf[:, c, KW - 1:UW], ba[:], vcol(f"bpw1{jb}", c), ctmp[tb], ALU.add, ALU.mult,
                    [bab, B_ctmp[tb], B_vecs], [B_u[c]])
        NDT = 6
        for c in range(KC):
            part, pb = cpart[c], B_cpart[c]
            wo = voff[f"wdw{jb}"] + c * KW
            ts("vector", part, ubuf[:, c, 0:T], vecs[:, wo:wo + 1], None, ALU.mult, None, [B_u[c], B_vecs], [pb])
            for k in range(1, NDT):
                stt(part, ubuf[:, c, k:k + T], vecs[:, wo + k:wo + k + 1], part, ALU.mult, ALU.add,
                    [B_u[c], B_vecs, pb], [pb])
            w, wb = wget()
            w3 = w[:, 0:KW * P].rearrange("p (k n) -> p k n", n=P)
            bank, bbuf = rot.next()
            for k in range(NDT, KW):
                mm(bank[:], w3[:, k, :], ubuf[:, c, k:k + T], k == NDT, k == KW - 1, [wb, B_u[c]], [bbuf])
            flush_stats()
            stt(zt[:, c, :], bank[:], vcol(f"bdw{jb}", c), part, ALU.add, ALU.add, [bbuf, B_vecs, pb], [B_z[c]])
            ln_feed(c)
        cp("gpsimd", halo[jb][:], ubuf[:, :, T:UW], B_u, [B_halo[jb]])
        ln_finish(f"blg{jb}", f"blb{jb}", "conv", cbuf, B_cb)
        n = 0
        for s in range(2):
            w, wb = wget()
            w3 = w[:].rearrange("p (k n) -> p k n", n=512)
            grp = [(j,) + rot.next() for j in range(4)]
            if s == 0:
                for k in range(KC):
                    for j, bank, bbuf in grp:
                        mm(bank[:], w3[:, k, j * P:(j + 1) * P], cbuf[:, k, :], k == 0, k == KC - 1, [wb, B_cb[k]], [bbuf])
            for j, bank, bbuf in grp:
                d = 4 * s + j
                if s != 0:
                    for k in range(KC):
                        mm(bank[:], w3[:, k, j * P:(j + 1) * P], cbuf[:, k, :], k == 0, k == KC - 1, [wb, B_cb[k]], [bbuf])
                flush_stats()
                tb = n % 3
                n += 1
                act(ctmp[tb], bank[:], AF.Identity, [bbuf, B_vecs], [B_ctmp[tb]], bias=vcol(f"bpw2{jb}", d))
                stt(zt[:, d, :], xres[:, d, :], ALPHA, ctmp[tb], ALU.mult, ALU.add, [B_xres[d], B_ctmp[tb]], [B_z[d]])
                ln_feed(d)
        ln_finish(f"lnmg{li}", f"lnmb{li}", "resid")

    def bview(i):
        return abf[:, i * KC * T:(i + 1) * KC * T].rearrange("p (c t) -> p c t", t=T)

    qt, qS, kt, khtok, vtok, gs, og = (bview(i) for i in range(7))
    khT = [abf[:, 7 * KC * T + i * T:7 * KC * T + (i + 1) * T] for i in range(2)]
    B_qt, B_qS, B_kt, B_khtok, B_vtok, B_gs, B_og = (bl(n, KC) for n in ("qt", "qS", "kt", "khtok", "vtok", "gs", "og"))
    B_khT = bl("khT", 2)
    zbq = [zb[:].rearrange("p c t -> p (c t)").bitcast(F32), zsq[:].rearrange("p c t -> p (c t)").bitcast(F32)]
    hf32 = [[af32[:, (par * 7 + i) * T:(par * 7 + i + 1) * T] for i in range(7)] for par in range(2)]
    hf32.append([zbq[i // 4][:, (i % 4) * T:(i % 4 + 1) * T] for i in range(7)])
    B_hf = [bl(f"hf{par}_", 7) for par in range(2)]
    B_hf.append([(B_zb[2 * i], B_zb[2 * i + 1]) if i < 4 else (B_zsq[2 * (i - 4)], B_zsq[2 * (i - 4) + 1])
                 for i in range(7)])
    rs4 = [af32[:, g * 7 * T:g * 7 * T + 4 * T].rearrange("p (h t) -> p h t", t=T) for g in range(2)]
    at2 = [abf[:, 7 * KC * T + 2 * T + i * T:7 * KC * T + 2 * T + (i + 1) * T] for i in range(2)]
    B_at = bl("at", 2)

    def hgrn_mixer(li, ja, lbi):
        S.soft_barrier()
        rot = Rot([0, 1, 2, 3, 4, 5, 6, 7])
        lb_ap = lbv[:, 2 * lbi, :]
        oml_ap = lbv[:, 2 * lbi + 1, :]
        St, Sb = Sst[ja], Sbf[ja]
        def bufs(h):
            par = h % 3
            return hf32[par], B_hf[par]

        def s0(h):
            (A_, K_, Bc, D1, E1, D2, QS), (bA, bK, bB, bD1, bE1, bD2, bQS) = bufs(h)
            w, wb = wget()
            w4 = w[:].rearrange("p (k s n) -> p k s n", s=4, n=P)
            bf_, bfb = rot.next()
            bq, bqb = rot.next()
            bg, bgb = rot.next()
            if h == 0:
                for k in range(KC):
                    mm(bf_[:], w4[:, k, 1, :], xbf[:, k, :], k == 0, k == KC - 1, [wb, B_xbf[k]], [bfb])
                    mm(bq[:], w4[:, k, 0, :], xbf[:, k, :], k == 0, k == KC - 1, [wb, B_xbf[k]], [bqb])
                    mm(bg[:], w4[:, k, 3, :], xbf[:, k, :], k == 0, k == KC - 1, [wb, B_xbf[k]], [bgb])
            else:
                for k in range(KC):
                    mm(bf_[:], w4[:, k, 1, :], xbf[:, k, :], k == 0, k == KC - 1, [wb, B_xbf[k]], [bfb])
            act(A_, bf_[:], AF.Sigmoid, [bfb], [bA])
            if h != 0:
                for k in range(KC):
                    mm(bq[:], w4[:, k, 0, :], xbf[:, k, :], k == 0, k == KC - 1, [wb, B_xbf[k]], [bqb])
            act(QS, bq[:], AF.Sigmoid, [bqb], [bQS])
            tt("vector", QS, bq[:], QS, ALU.mult, [bqb, bQS], [bQS])
            if h != 0:
                for k in range(KC):
                    mm(bg[:], w4[:, k, 3, :], xbf[:, k, :], k == 0, k == KC - 1, [wb, B_xbf[k]], [bgb])
            act(E1, bg[:], AF.Sigmoid, [bgb], [bE1])
            tt("vector", gs[:, h, :], bg[:], E1, ALU.mult, [bgb, bE1], [B_gs[h]])
            bv, bvb = rot.next()
            for tc in range(4):
                for k in range(KC):
                    mm(bv[:, tc * P:(tc + 1) * P], xbf[:, k, tc * P:(tc + 1) * P], w4[:, k, 2, :], k == 0, k == KC - 1,
                       [wb, B_xbf[k]], [bvb])
            cp("vector", vtok[:, h, :], bv[:], [bvb], [B_vtok[h]])

        def s1a(h):
            (A_, K_, Bc, D1, E1, D2, QS), (bA, bK, bB, bD1, bE1, bD2, bQS) = bufs(h)
            ts("vector", A_, A_, oml_ap[:, h:h + 1], lb_ap[:, h:h + 1], ALU.mult, ALU.add, [bA, B_lb], [bA])
            ts("gpsimd", K_, A_, -1.0, 1.0, ALU.mult, ALU.add, [bA], [bK])
            ts("vector", A_, A_, GATE_EPS, None, ALU.max, None, [bA], [bA])
            act(A_, A_, AF.Ln, [bA], [bA])

        def s1b(h):
            (A_, K_, Bc, D1, E1, D2, QS), (bA, bK, bB, bD1, bE1, bD2, bQS) = bufs(h)
            S.add("vector", (lambda Bc, A_: lambda e: e.tensor_tensor_scan(out=Bc, data0=carry[:], data1=A_, initial=0.0,
                                                                            op0=ALU.mult, op1=ALU.add))(Bc, A_),
                  [bA, B_const], [bB])
            b3 = Bc.rearrange("p (c j) -> p c j", j=CH)
            tt("gpsimd", D1.rearrange("p (c j) -> p c j", j=CH), b3, b3[:, :, CH // 2 - 1:CH // 2].to_broadcast([P, NCH, CH]),
               ALU.subtract, [bB], [bD1])
            tt("gpsimd", D2.rearrange("p (c j) -> p c j", j=CH), b3, b3[:, :, CH - 1:CH].to_broadcast([P, NCH, CH]),
               ALU.subtract, [bB], [bD2])

        def s2a(h):
            (A_, K_, Bc, D1, E1, D2, QS), (bA, bK, bB, bD1, bE1, bD2, bQS) = bufs(h)
            act(E1, D1, AF.Exp, [bD1], [bE1])
            act(D1, D1, AF.Exp, [bD1], [bD1], scale=-1.0)
            act(D2, D2, AF.Exp, [bD2], [bD2], scale=-1.0)
            act(Bc, Bc, AF.Exp, [bB], [bB])

        def s2b(h):
            (A_, K_, Bc, D1, E1, D2, QS), (bA, bK, bB, bD1, bE1, bD2, bQS) = bufs(h)
            b3 = Bc.rearrange("p (c j) -> p c j", j=CH)
            tt("vector", qt[:, h, :], QS, E1, ALU.mult, [bQS, bE1], [B_qt[h]])
            tt("vector", kt[:, h, :], K_, D1, ALU.mult, [bK, bD1], [B_kt[h]])
            kp = h % 2
            tt("gpsimd", khT[kp], K_, D2, ALU.mult, [bK, bD2], [B_khT[kp]])
            cp("gpsimd", dec[:, h, :], b3[:, :, CH - 1:CH].rearrange("p c o -> p (c o)"), [bB], [B_dec[h]])
            tt("gpsimd", qS[:, h, :], QS, Bc, ALU.mult, [bQS, bB], [B_qS[h]])
            bk, bkb = rot.next()
            for tc in range(4):
                mm(bk[:, tc * P:(tc + 1) * P], khT[kp][:, tc * P:(tc + 1) * P], ident_b[:], True, True,
                   [B_khT[kp], B_const], [bkb])
            cp("vector", khtok[:, h, :], bk[:], [bkb], [B_khtok[h]])

        for step in range(KC + 2):
            h1, h2 = step - 1, step - 2
            if 0 <= h2 < KC:
                s2a(h2)
            if 0 <= h1 < KC:
                s1a(h1)
                s1b(h1)
            if step < KC:
                s0(step)
            if 0 <= h2 < KC:
                s2b(h2)
        for c in range(NCH):
            blk, hf = c // 2, c % 2
            r0, r1 = hf * CH, (hf + 1) * CH
            par = c % 2
            X, Xb = banks[3 + par], bank_buf[3 + par]
            Y, Yb = banks[5 + par], bank_buf[5 + par]
            Zs = [(banks[0], bank_buf[0]), (banks[1], bank_buf[1])] if par == 0 else \
                 [(banks[2], bank_buf[2]), (banks[7], bank_buf[7])]
            a_sb, a_b = at2[par], B_at[par]
            for h in range(KC):
                mm(X[:, h * CH:(h + 1) * CH], kt[:, h, blk * P:(blk + 1) * P], qt[:, h, c * CH:(c + 1) * CH], True, True,
                   [B_kt[h], B_qt[h]], [Xb])
            tt("vector", a_sb[r0:r1, :].rearrange("p (h t) -> p h t", t=CH),
               X[r0:r1, :].rearrange("p (h t) -> p h t", t=CH),
               cmask[r0:r1, 0:1, :].to_broadcast([CH, KC, CH]), ALU.mult, [Xb, B_const], [a_b])
            for h in range(KC):
                Z, Zb = Zs[h // 4]
                hh = h % 4
                mm(Z[:, hh * P:(hh + 1) * P], khtok[r0:r1, h, blk * P:(blk + 1) * P],
                   vtok[r0:r1, h, blk * P:(blk + 1) * P], True, True, [B_khtok[h], B_vtok[h]], [Zb])
            for h in range(KC):
                mm(Y[:, h * CH:(h + 1) * CH], vtok[r0:r1, h, blk * P:(blk + 1) * P], a_sb[r0:r1, h * CH:(h + 1) * CH],
                   True, False, [B_vtok[h], a_b], [Yb])
                mm(Y[:, h * CH:(h + 1) * CH], Sb[:, h, :], qS[:, h, c * CH:(c + 1) * CH], False, True,
                   [B_Sbg[ja][h // 4], B_qS[h]], [Yb])
            act(zt[:, :, c * CH:(c + 1) * CH], Y[:].rearrange("p (h t) -> p h t", t=CH), AF.Copy, [Yb], B_z)
            for g in range(2):
                Z, Zb = Zs[g]
                for hh in range(4):
                    h = 4 * g + hh
                    stt(St[:, h, :], St[:, h, :], dec[:, h, c:c + 1], Z[:, hh * P:(hh + 1) * P], ALU.mult, ALU.add,
                        [B_S[ja][h], B_dec[h], Zb], [B_S[ja][h]])
                act(Sb[:, 4 * g:4 * g + 4, :], St[:, 4 * g:4 * g + 4, :], AF.Copy, B_S[ja][4 * g:4 * g + 4], [B_Sbg[ja][g]])
        ango = voff[f"ang{ja}"]
        for g in range(2):
            hs = slice(4 * g, 4 * g + 4)
            for hh in range(4):
                h = 4 * g + hh
                act(zsq[:, h, :], zt[:, h, :], AF.Square, [B_z[h]], [B_zsq[h]])
                mm(banks[h][:], ones_h[:], zsq[:, h, :], True, True, [B_zsq[h], B_const], [bank_buf[h]])
            rb = B_hf[g][0:4]
            act(rs4[g], ps_all[:, hs, :], AF.Ln, bank_buf[hs] + [B_const], rb, bias=epsc[:, 1:2])
            act(rs4[g], rs4[g], AF.Exp, rb, rb, scale=-0.5)
            for hh in range(4):
                h = 4 * g + hh
                stt(zt[:, h, :], zt[:, h, :], vecs[:, ango + h:ango + h + 1], gs[:, h, :], ALU.mult, ALU.mult,
                    [B_z[h], B_gs[h], B_vecs], [B_z[h]])
            tt("vector", og[:, hs, :], zt[:, hs, :], rs4[g], ALU.mult, B_z[hs] + rb, B_og[hs])

        rot = Rot([0, 1, 2, 3, 4, 5])
        for s in range(2):
            w, wb = wget()
            w3 = w[:].rearrange("p (k n) -> p k n", n=512)
            grp = [(j,) + rot.next() for j in range(4)]
            if s == 0:
                for k in range(KC):
                    for j, bank, bbuf in grp:
                        mm(bank[:], w3[:, k, j * P:(j + 1) * P], og[:, k, :], k == 0, k == KC - 1, [wb, B_og[k]], [bbuf])
            for j, bank, bbuf in grp:
                d = 4 * s + j
                if s != 0:
                    for k in range(KC):
                        mm(bank[:], w3[:, k, j * P:(j + 1) * P], og[:, k, :], k == 0, k == KC - 1, [wb, B_og[k]], [bbuf])
                flush_stats()
                resid_from_psum(d, bank, bbuf)
        ln_finish(f"lnmg{li}", f"lnmb{li}", "resid")

    for lst in (B_hT, B_ftmp, B_u, B_cb, B_ctmp, B_cpart, B_qt, B_qS, B_kt, B_khtok, B_vtok, B_gs, B_og, B_khT,
                B_hf[0], B_hf[1], B_at):
        for b_ in lst:
            b_.arena = True

    xres_flat = xres[:].rearrange("p c t -> p (c t)")
    for it in range(n_tiles):
        for d in range(KC):
            src = zbq[d // 4][:, (d % 4) * T:(d % 4 + 1) * T]
            sbuf_ = (B_zb if d < 4 else B_zsq)[2 * (d % 4):2 * (d % 4) + 2]
            cp("vector" if d % 2 == 0 else "scalar", xbf[:, d, :], src, sbuf_, [B_xbf[d]])
        for d in range(KC):
            src = zbq[d // 4][:, (d % 4) * T:(d % 4 + 1) * T]
            sbuf_ = (B_zb if d < 4 else B_zsq)[2 * (d % 4):2 * (d % 4) + 2]
            cp("gpsimd" if d % 2 == 0 else "scalar", xres[:, d, :], src, sbuf_, [B_xres[d]])
        ja = jb = 0
        for li, k in enumerate(kinds):
            if it == 0:
                for pi, l2 in piece_layer:
                    if l2 == li + 1:
                        issue_cast(pi)
            if k == "A":
                hgrn_mixer(li, ja, ja)
                ja += 1
            else:
                conv_mixer(li, jb)
                jb += 1
            if li == L - 1 and it + 1 < n_tiles:
                ffn(li, hook=(lambda it: lambda: issue_xload(it + 1))(it))
            else:
                ffn(li)
        S.add("gpsimd", (lambda it: lambda e: e.dma_start(out=out_d[it], in_=xres_flat))(it), B_xres, (), dma_key="xout")
    S.add("gpsimd", lambda e: None, (), (), extra=[S.dma_last["xout"]])
    S.finalize()

    eng_sems = {e: es.enter_context(nc.semaphore(f"sem_{e}")) for e in ENGS}
    dma_sems = {k: es.enter_context(nc.semaphore(f"dsem_{k}")) for k in S.dma_count}
    with nc.Block() as block:
        @block.sync
        def _(e):
            S.emit("sync", e, eng_sems, dma_sems)

        @block.tensor
        def _(e):
            S.emit("tensor", e, eng_sems, dma_sems)

        @block.vector
        def _(e):
            S.emit("vector", e, eng_sems, dma_sems)

        @block.scalar
        def _(e):
            S.emit("scalar", e, eng_sems, dma_sems)

        @block.gpsimd
        def _(e):
            S.emit("gpsimd", e, eng_sems, dma_sems)
    es.close()
    return nc, S


def x_to_fm(xb, n_tiles):
    a = xb.reshape(n_tiles, T, KC, P).transpose(0, 3, 2, 1)
    return np.ascontiguousarray(a).reshape(n_tiles, P, KC * T)


def fm_to_x(o, n_tiles):
    a = o.reshape(n_tiles, P, KC, T).transpose(0, 3, 2, 1)
    return np.ascontiguousarray(a).reshape(n_tiles * T, D)


def run(x, inp, kinds, layer_ids, core_ids, placement=None):
    x = np.asarray(x, np.float32)
    nb, seqlen, _ = x.shape
    n_tiles = seqlen // T
    wslabs, vecs, seq, nh = pack_weights(kinds, layer_ids, inp)
    nc, _ = build(n_tiles, kinds)
    if placement is None:
        placement = list(range(nb))
    ncore = len(core_ids)
    in_maps = [None] * ncore
    for b, c in enumerate(placement):
        in_maps[c] = {"x_fm": x_to_fm(x[b], n_tiles), "wslabs": wslabs, "vecs": vecs}
    idle = None
    for c in range(ncore):
        if in_maps[c] is None:
            if idle is None:
                idle = {"x_fm": np.zeros((n_tiles, P, KC * T), np.float32), "wslabs": np.zeros_like(wslabs),
                        "vecs": np.zeros_like(vecs)}
            in_maps[c] = idle
    res = run_bass_kernel_spmd(nc, in_maps, core_ids=core_ids)
    return np.stack([fm_to_x(np.asarray(res.results[c]["out_fm"]), n_tiles) for c in placement])


def kernel(**inputs):
    x = np.asarray(inputs["x"], np.float32)
    kinds = ["A", "B", "A", "B"]
    out = run(x, inputs, kinds, [0, 1, 2, 3], list(range(x.shape[0])))
    return out.astype(np.float32)
```
